# Optimizing a Trainium2 kernel written in Bass

```python
import jax, jax.numpy as jnp
from jax import lax
import numpy as np

D_MODEL = 2048
BATCH = 2
SEQ = 16384
DEPTH = 1

MEM_LEN = 256
HEAD_DIM = 128
MOBA_HEADS = 8
MOBA_W = MOBA_HEADS * HEAD_DIM
MOBA_BLOCK = 256
MOBA_TOPK = 3
MOBA_Q_CHUNK = 64
GMLP_GROUPS = 4
GMLP_W = GMLP_GROUPS * HEAD_DIM
GMLP_CHUNK = 128
MEM_HEADS = 4
MEM_W = MEM_HEADS * HEAD_DIM
D_MIX = MOBA_W + GMLP_W + MEM_W
IN_SPLITS = (MOBA_W, MOBA_W, MOBA_W, MOBA_W, GMLP_W, GMLP_W, GMLP_W, MEM_W, MEM_W)
D_IN = sum(IN_SPLITS)
IN_SPLIT_POINTS = tuple(int(c) for c in np.cumsum(IN_SPLITS)[:-1])
ROPE_THETA = 10000.0
LN_EPS = 1e-5
DEEPNORM_ALPHA = (2.0 * DEPTH) ** 0.25
DEEPNORM_BETA = (8.0 * DEPTH) ** -0.25
NEG_INF = -1e30

kernel_name = "hymba_moba_gmlp_memxattn_deepnorm"


def layer_norm(x, g, b):
    xf = x.astype(jnp.float32)
    mu = xf.mean(-1, keepdims=True)
    var = jnp.square(xf - mu).mean(-1, keepdims=True)
    return ((xf - mu) * lax.rsqrt(var + LN_EPS) * g.astype(jnp.float32) + b.astype(jnp.float32)).astype(x.dtype)


def rope(x, positions):
    half = HEAD_DIM // 2
    inv_freq = ROPE_THETA ** (-jnp.arange(half, dtype=jnp.float32) / half)
    ang = positions.astype(jnp.float32)[..., None] * inv_freq
    cos = jnp.cos(ang)[:, :, None, :]
    sin = jnp.sin(ang)[:, :, None, :]
    xf = x.astype(jnp.float32)
    x1, x2 = xf[..., :half], xf[..., half:]
    return jnp.concatenate([x1 * cos - x2 * sin, x2 * cos + x1 * sin], axis=-1).astype(x.dtype)


def moba_attention(q, k, v):
    b, s, h, d = q.shape
    s_pad = -(-s // MOBA_BLOCK) * MOBA_BLOCK
    pad = ((0, 0), (0, s_pad - s), (0, 0), (0, 0))
    q, k, v = (jnp.pad(t, pad).transpose(0, 2, 1, 3) for t in (q, k, v))
    nb = s_pad // MOBA_BLOCK
    n_sel = min(MOBA_TOPK, nb)
    kb = k.reshape(b, h, nb, MOBA_BLOCK, d)
    vb = v.reshape(b, h, nb, MOBA_BLOCK, d)
    k_mean = kb.astype(jnp.float32).mean(axis=3)
    q_block = jnp.arange(s_pad) // MOBA_BLOCK
    gate = jnp.einsum('bhsd,bhnd->bhsn', q.astype(jnp.float32), k_mean)
    past = jnp.arange(nb)[None, :] < q_block[:, None]
    gate = jnp.where(past, gate, NEG_INF)
    _, sel_idx = lax.top_k(gate, n_sel)
    sel_valid = sel_idx < q_block[:, None]
    scale = d ** -0.5
    bi = jnp.arange(b)[:, None, None, None]
    hi = jnp.arange(h)[None, :, None, None]
    key_off = jnp.arange(MOBA_BLOCK)
    q_off = jnp.arange(MOBA_Q_CHUNK)
    n_keys_sel = n_sel * MOBA_BLOCK

    def chunk(c):
        start = c * MOBA_Q_CHUNK
        q_c = lax.dynamic_slice_in_dim(q, start, MOBA_Q_CHUNK, axis=2)
        idx_c = lax.dynamic_slice_in_dim(sel_idx, start, MOBA_Q_CHUNK, axis=2)
        val_c = lax.dynamic_slice_in_dim(sel_valid, start, MOBA_Q_CHUNK, axis=2)
        blk = start // MOBA_BLOCK
        k_own = lax.dynamic_index_in_dim(kb, blk, axis=2, keepdims=False)
        v_own = lax.dynamic_index_in_dim(vb, blk, axis=2, keepdims=False)
        k_sel = kb[bi, hi, idx_c]
        v_sel = vb[bi, hi, idx_c]
        l_sel = jnp.einsum('bhqd,bhqnkd->bhqnk', q_c, k_sel).astype(jnp.float32) * scale
        l_sel = jnp.where(val_c[..., None], l_sel, NEG_INF).reshape(b, h, MOBA_Q_CHUNK, n_keys_sel)
        l_own = jnp.einsum('bhqd,bhkd->bhqk', q_c, k_own).astype(jnp.float32) * scale
        causal = (blk * MOBA_BLOCK + key_off)[None, :] <= (start + q_off)[:, None]
        l_own = jnp.where(causal, l_own, NEG_INF)
        p = jax.nn.softmax(jnp.concatenate([l_sel, l_own], axis=-1), axis=-1).astype(v.dtype)
        p_sel = p[..., :n_keys_sel].reshape(b, h, MOBA_Q_CHUNK, n_sel, MOBA_BLOCK)
        p_own = p[..., n_keys_sel:]
        return (jnp.einsum('bhqnk,bhqnkd->bhqd', p_sel, v_sel)
                + jnp.einsum('bhqk,bhkd->bhqd', p_own, v_own))

    out = lax.map(chunk, jnp.arange(s_pad // MOBA_Q_CHUNK))
    out = out.transpose(1, 0, 3, 2, 4).reshape(b, s_pad, h, d)
    return out[:, :s]


def gmlp_spatial_gate(u, v, ln_g, ln_b, w_s, b_s):
    u = jax.nn.gelu(u)
    v = layer_norm(jax.nn.gelu(v), ln_g, ln_b)
    b, s, _ = v.shape
    v = v.reshape(b, s // GMLP_CHUNK, GMLP_CHUNK, GMLP_GROUPS, HEAD_DIM)
    tril = jnp.tril(jnp.ones((GMLP_CHUNK, GMLP_CHUNK), dtype=bool))
    w = jnp.where(tril, w_s, 0)
    mixed = jnp.einsum('gts,bnsgc->bntgc', w, v) + b_s.T[None, None, :, :, None]
    return u * mixed.reshape(b, s, GMLP_W)


def memory_attention(q, mem_k, mem_v):
    logits = jnp.einsum('bshd,bmhd->bhsm', q, mem_k).astype(jnp.float32) * (HEAD_DIM ** -0.5)
    p = jax.nn.softmax(logits, axis=-1).astype(mem_v.dtype)
    return jnp.einsum('bhsm,bmhd->bshd', p, mem_v)


def setup_inputs(seed: int = 0) -> dict:
    key = jax.random.key(seed)
    ks = jax.random.split(key, 12)
    f32 = jnp.float32
    x = jax.random.normal(ks[0], (BATCH, SEQ, D_MODEL), f32)
    mem = jax.random.normal(ks[1], (BATCH, MEM_LEN, D_MODEL), f32)
    positions = jnp.broadcast_to(jnp.arange(SEQ, dtype=jnp.int32), (BATCH, SEQ))
    w_in = jax.random.normal(ks[2], (DEPTH, D_MODEL, D_IN), f32) * D_MODEL ** -0.5
    w_mem_kv = jax.random.normal(ks[3], (DEPTH, D_MODEL, 2 * MEM_W), f32) * D_MODEL ** -0.5
    gmlp_ln_g = 1.0 + 0.02 * jax.random.normal(ks[4], (DEPTH, GMLP_W), f32)
    gmlp_ln_b = 0.02 * jax.random.normal(ks[5], (DEPTH, GMLP_W), f32)
    gmlp_w_s = jax.random.normal(ks[6], (DEPTH, GMLP_GROUPS, GMLP_CHUNK, GMLP_CHUNK), f32) * GMLP_CHUNK ** -0.5
    gmlp_b_s = 1.0 + 0.1 * jax.random.normal(ks[7], (DEPTH, GMLP_GROUPS, GMLP_CHUNK), f32)
    w_out = jax.random.normal(ks[8], (DEPTH, D_MIX, D_MODEL), f32) * (D_MIX ** -0.5) * DEEPNORM_BETA
    ln_g = 1.0 + 0.02 * jax.random.normal(ks[9], (DEPTH, D_MODEL), f32)
    ln_b = 0.02 * jax.random.normal(ks[10], (DEPTH, D_MODEL), f32)
    return {"x": x, "mem": mem, "positions": positions, "w_in": w_in, "w_mem_kv": w_mem_kv,
            "gmlp_ln_g": gmlp_ln_g, "gmlp_ln_b": gmlp_ln_b, "gmlp_w_s": gmlp_w_s,
            "gmlp_b_s": gmlp_b_s, "w_out": w_out, "ln_g": ln_g, "ln_b": ln_b}


def reference(x, mem, positions, w_in, w_mem_kv, gmlp_ln_g, gmlp_ln_b, gmlp_w_s, gmlp_b_s,
              w_out, ln_g, ln_b):
    b, s, _ = x.shape
    for l in range(DEPTH):
        proj = jnp.einsum('bsd,de->bse', x, w_in[l])
        q_mo, k_mo, v_mo, g_mo, u_gm, v_gm, g_gm, q_me, g_me = jnp.split(proj, IN_SPLIT_POINTS, axis=-1)
        q_mo = rope(q_mo.reshape(b, s, MOBA_HEADS, HEAD_DIM), positions)
        k_mo = rope(k_mo.reshape(b, s, MOBA_HEADS, HEAD_DIM), positions)
        v_mo = v_mo.reshape(b, s, MOBA_HEADS, HEAD_DIM)
        y_mo = moba_attention(q_mo, k_mo, v_mo).reshape(b, s, MOBA_W) * jax.nn.silu(g_mo)
        y_gm = gmlp_spatial_gate(u_gm, v_gm, gmlp_ln_g[l], gmlp_ln_b[l], gmlp_w_s[l], gmlp_b_s[l]) * jax.nn.silu(g_gm)
        mem_k, mem_v = jnp.split(jnp.einsum('bmd,de->bme', mem, w_mem_kv[l]), 2, axis=-1)
        y_me = memory_attention(q_me.reshape(b, s, MEM_HEADS, HEAD_DIM),
                                mem_k.reshape(b, MEM_LEN, MEM_HEADS, HEAD_DIM),
                                mem_v.reshape(b, MEM_LEN, MEM_HEADS, HEAD_DIM)).reshape(b, s, MEM_W) * jax.nn.silu(g_me)
        y = jnp.concatenate([y_mo, y_gm, y_me], axis=-1)
        sub = jnp.einsum('bse,ed->bsd', y, w_out[l])
        x = layer_norm(DEEPNORM_ALPHA * x + sub, ln_g[l], ln_b[l])
    return x
```

```python
import contextlib
import numpy as np
import concourse.bass as bass
import concourse.mybir as mybir
from concourse.bass_utils import run_bass_kernel_spmd

F32 = mybir.dt.float32
BF16 = mybir.dt.bfloat16
I32 = mybir.dt.int32
ALU = mybir.AluOpType
AF = mybir.ActivationFunctionType
AX = mybir.AxisListType

NEG = -30000.0
SCALE = 128.0 ** -0.5
ALPHA = 2.0 ** 0.25
LN_EPS = 1e-5
MAGIC = 12582912.0
TWO_PI = 2.0 * np.pi
C1 = 6.28125
C2 = float(TWO_PI - C1)
ARENA_WORDS = 49 * 1024


class Op:
    __slots__ = ("eng", "fn", "reads", "writes", "semkey", "waits", "needs_inc", "val", "is_dma", "extra", "ndma")

    def __init__(self, eng, fn, reads, writes, semkey, ndma=1):
        self.ndma = ndma
        self.eng = eng
        self.fn = fn
        self.reads = reads
        self.writes = writes
        self.semkey = semkey
        self.is_dma = semkey is not None
        self.waits = []
        self.needs_inc = False
        self.val = None
        self.extra = ()


class Prog:
    ENGS = ("pe", "act", "dve", "pool", "sp")

    def __init__(self, nc):
        self.nc = nc
        self.ops = []
        self.last_eng = {}
        self.last_dma = {}

    def op(self, eng, fn, reads=(), writes=(), semkey=None, ndma=1):
        o = Op(eng, fn, tuple(reads), tuple(writes), semkey, ndma)
        self.ops.append(o)
        if semkey is None:
            if fn is not None:
                self.last_eng[eng] = o
        else:
            self.last_dma[semkey] = o
        return o

    def dma(self, q, out, in_, reads=(), writes=(), semkey=None):
        assert semkey is not None
        return self.op(q, lambda e: e.dma_start(out=out, in_=in_), reads, writes, semkey)

    def barrier(self):
        pend = list(self.last_eng.values()) + list(self.last_dma.values())
        for eng in self.ENGS:
            o = self.op(eng, None)
            o.extra = tuple(pend)

    def analyze(self):
        last_writer = {}
        readers = {}
        for o in self.ops:
            deps = list(o.extra)
            for k in o.reads:
                w = last_writer.get(k)
                if w is not None:
                    deps.append(w)
            for k in o.writes:
                w = last_writer.get(k)
                if w is not None:
                    deps.append(w)
                deps.extend(readers.get(k, {}).values())
            for k in o.reads:
                rk = readers.setdefault(k, {})
                rk[(o.eng, id(o)) if o.is_dma else o.eng] = o
            for k in o.writes:
                last_writer[k] = o
                readers[k] = {}
            seen = set()
            for p in deps:
                if p is o or id(p) in seen:
                    continue
                seen.add(id(p))
                if p.eng == "pe" and o.eng == "pe" and not p.is_dma and not o.is_dma:
                    continue
                o.waits.append(p)
                p.needs_inc = True
        cnt = {}
        for o in self.ops:
            if o.is_dma:
                key = ("dma", o.semkey)
                cnt[key] = cnt.get(key, 0) + 16 * o.ndma
                o.val = (key, cnt[key])
            elif o.needs_inc:
                key = ("eng", o.eng)
                cnt[key] = cnt.get(key, 0) + 1
                o.val = (key, cnt[key])
        self.semkeys = sorted(cnt.keys(), key=str)

    def emit(self):
        nc = self.nc
        self.analyze()
        with contextlib.ExitStack() as st:
            sems = {}
            for i, k in enumerate(self.semkeys):
                sems[k] = st.enter_context(nc.semaphore("s%d" % i))
            block = st.enter_context(nc.Block())
            streams = {e: [] for e in self.ENGS}
            for o in self.ops:
                streams[o.eng].append(o)

            def run_stream(name):
                def body(e):
                    known = {}
                    for o in streams[name]:
                        need = {}
                        for p in o.waits:
                            k, v = p.val
                            if v > need.get(k, 0):
                                need[k] = v
                        for k, v in need.items():
                            if known.get(k, 0) >= v:
                                continue
                            e.wait_ge(sems[k], v)
                            known[k] = v
                        if o.fn is None:
                            continue
                        ins = o.fn(e)
                        if o.val is not None:
                            if isinstance(ins, (list, tuple)):
                                for i_ in ins:
                                    i_.then_inc(sems[o.val[0]], 16)
                            else:
                                ins.then_inc(sems[o.val[0]], 16 if o.is_dma else 1)
                return body

            block.tensor(run_stream("pe"))
            block.scalar(run_stream("act"))
            block.vector(run_stream("dve"))
            block.gpsimd(run_stream("pool"))
            block.sync(run_stream("sp"))


class Arena:
    def __init__(self, ap):
        self.ap = ap
        self.off = 0

    def reset(self, off=0):
        self.off = off

    def f32(self, n, parts=128):
        a = self.ap[0:parts, self.off:self.off + n]
        self.off += n
        assert self.off <= ARENA_WORDS, "SBUF arena overflow %d" % self.off
        return a

    def bf16(self, n, parts=128):
        assert n % 2 == 0
        return self.f32(n // 2, parts).bitcast(BF16)

    def i32(self, n, parts=128):
        return self.f32(n, parts).bitcast(I32)


def build(T, debug=False):
    NB = T // 256
    NI = NB // 4
    NPAIR = NI // 2
    NOWN = T // 4
    NTA = T // 512
    NTB = NOWN // 512
    assert NI % 2 == 0 and NB <= 64

    nc = bass.Bass("TRN2", target_bir_lowering=False)

    def din(name, shape, dt=F32):
        return nc.dram_tensor(name, shape, dt, kind="ExternalInput").ap()

    xT_all = din("xT_all", [2048, T])
    xT_own = din("xT_own", [2048, NOWN])
    x_own = din("x_own", [NOWN, 2048])
    pos_all = din("pos_all", [1, T], I32)
    pos_own = din("pos_own", [1, NOWN], I32)
    w_in = din("w_in", [2048, 6656])
    w_out = din("w_out", [2048, 2048])
    w_mkv = din("w_mkv", [2048, 1024])
    memT = din("memT", [2048, 256])
    glng = din("glng", [1, 512])
    glnb = din("glnb", [1, 512])
    wsT = din("wsT", [128, 512])
    bs = din("bs", [1, 512])
    lng = din("lng", [1, 2048])
    lnb = din("lnb", [1, 2048])
    invf = din("invf", [128, 1])
    pastneg = din("pastneg", [1, NTB * 4 * NB])
    notown = din("notown", [1, NTB * 4 * NB])
    m2d = din("m2d", [128, 8 * 2 * 512])
    out = nc.dram_tensor("out", [NOWN, 2048], F32, kind="ExternalOutput").ap()

    skind = "ExternalOutput" if debug else "Internal"
    Kscr = nc.dram_tensor("Kscr", [8, 128, T], BF16, kind=skind).ap()
    Vscr = nc.dram_tensor("Vscr", [T, 1024], BF16, kind=skind).ap()
    Qscr = nc.dram_tensor("Qscr", [8, 128, NOWN], BF16, kind=skind).ap()
    Gscr = nc.dram_tensor("Gscr", [8, 128, NOWN], BF16, kind=skind).ap()
    RTscr = nc.dram_tensor("RTscr", [8, NB, NOWN], BF16, kind=skind).ap()
    Yscr = nc.dram_tensor("Yscr", [2048, NOWN], BF16, kind=skind).ap()

    with contextlib.ExitStack() as st:
        arena_t = st.enter_context(nc.sbuf_tensor("arena", [128, ARENA_WORDS], F32))
        A = Arena(arena_t[:])
        ps_all = st.enter_context(nc.psum_tensor("ps_all", [128, 4096], F32))[:]
        pbank = [ps_all[:, i * 512:(i + 1) * 512] for i in range(8)]
        P = Prog(nc)
        uid = [0]

        def U(prefix):
            uid[0] += 1
            return (prefix, uid[0])

        ident = A.bf16(128)
        ones_bf = A.bf16(128)
        ones_f = A.f32(128)
        invf_s = A.f32(1)
        eps_s = A.f32(1)
        kmean = A.f32(8 * NB)
        kmean_bf = A.bf16(8 * NB)
        PERSIST = A.off

        P.op("pool", lambda e: e.memset(ident, 1.0), [], ["ident"])
        P.op("pool", lambda e: e.affine_select(out=ident, in_=ident, pattern=[[-1, 128]], compare_op=ALU.is_equal,
                                               fill=0.0, base=0, channel_multiplier=1), ["ident"], ["ident"])
        P.op("pool", lambda e: e.memset(ones_bf, 1.0), [], ["ones_bf"])
        P.op("pool", lambda e: e.memset(ones_f, 1.0), [], ["ones_f"])
        P.op("pool", lambda e: e.memset(eps_s, LN_EPS), [], ["eps"])
        P.dma("sp", invf_s, invf, writes=["invf"], semkey="invf")

        cast_rr = [0]

        def cast(dst, src, reads, writes, engs=("act", "pool")):
            e_ = engs[cast_rr[0] % len(engs)]
            cast_rr[0] += 1
            if e_ == "act":
                P.op("act", lambda e: e.copy(out=dst, in_=src), reads, writes)
            else:
                P.op(e_, lambda e: e.tensor_copy(out=dst, in_=src), reads, writes)

        class Stage:
            def __init__(self, n):
                self.slots = [A.f32(2048) for _ in range(n)]
                self.i = 0

            def next(self):
                s = self.i % len(self.slots)
                self.i += 1
                return s, self.slots[s]

        def load_weights(stage, dst, w_dram, col0, ncols, wkey):
            dv = dst.rearrange("p (c n) -> p c n", c=16)
            for dc in range(16):
                c0 = 0
                while c0 < ncols:
                    n = min(2048, ncols - c0)
                    s, buf = stage.next()
                    P.dma("sp", buf[:, 0:n], w_dram[dc * 128:(dc + 1) * 128, col0 + c0:col0 + c0 + n],
                          writes=[("stg", s)], semkey=("stg", s))
                    cast(dv[:, dc, c0:c0 + n], buf[:, 0:n], [("stg", s)], [(wkey, dc, c0)], engs=("act", "pool", "dve"))
                    c0 += n
            return [(wkey, dc, c0) for dc in range(16) for c0 in range(0, ncols, 2048)]

        def load_xT(stage, xb, src, t0, xkey, engs=("act",)):
            sv = src.rearrange("(c p) t -> p c t", p=128)
            xv = xb.rearrange("p (c t) -> p c t", c=16)
            for qd in range(4):
                s, buf = stage.next()
                bv = buf.rearrange("p (c t) -> p c t", c=4)
                P.dma("sp", bv, sv[:, qd * 4:(qd + 1) * 4, t0:t0 + 512], writes=[("stg", s)], semkey=("stg", s))
                cast(xv[:, qd * 4:(qd + 1) * 4, :], bv, [("stg", s)], [tuple(xkey) + (qd,)], engs=engs)
            return [tuple(xkey) + (qd,) for qd in range(4)]

        class RopeTab:
            def __init__(self):
                self.posb = A.i32(512)
                self.posf = A.f32(512)
                self.ang = [A.f32(512) for _ in range(2)]
                self.t2 = [A.f32(512) for _ in range(2)]
                self.r = [A.f32(512) for _ in range(2)]
                self.cs = [A.f32(512) for _ in range(2)]
                self.sn = [A.f32(512) for _ in range(2)]

            def make(self, pos_dram, t0, slot):
                T_ = self
                P.dma("sp", T_.posb, pos_dram[:, t0:t0 + 512].partition_broadcast(128), writes=["posb"], semkey="posb")
                P.op("dve", lambda e: e.tensor_copy(out=T_.posf, in_=T_.posb), ["posb"], ["posf"])
                P.op("dve", lambda e: e.tensor_scalar(out=T_.ang[0], in0=T_.posf, scalar1=invf_s[:, 0:1], scalar2=None,
                                                      op0=ALU.mult), ["posf", "invf"], [("ang", 0)])
                P.op("dve", lambda e: e.tensor_scalar(out=T_.ang[1], in0=T_.ang[0], scalar1=float(np.pi / 2), scalar2=None,
                                                      op0=ALU.add), [("ang", 0)], [("ang", 1)])
                for w in (0, 1):
                    ang, t2, r = T_.ang[w], T_.t2[w], T_.r[w]
                    P.op("dve", lambda e, ang=ang, t2=t2: e.tensor_scalar(out=t2, in0=ang, scalar1=float(1.0 / TWO_PI), scalar2=MAGIC,
                                                                          op0=ALU.mult, op1=ALU.add), [("ang", w)], [("t2", w)])
                    P.op("dve", lambda e, t2=t2: e.tensor_scalar(out=t2, in0=t2, scalar1=-MAGIC, scalar2=None, op0=ALU.add),
                         [("t2", w)], [("t2", w)])
                    P.op("dve", lambda e, ang=ang, t2=t2, r=r: e.scalar_tensor_tensor(out=r, in0=t2, scalar=-C1, in1=ang, op0=ALU.mult,
                                                                                      op1=ALU.add), [("t2", w), ("ang", w)], [("r", w)])
                    P.op("dve", lambda e, t2=t2, r=r: e.scalar_tensor_tensor(out=r, in0=t2, scalar=-C2, in1=r, op0=ALU.mult,
                                                                             op1=ALU.add), [("t2", w), ("r", w)], [("r", w)])
                    P.op("dve", lambda e, r=r: e.tensor_scalar(out=r, in0=r, scalar1=float(np.pi), scalar2=float(-np.pi),
                                                               op0=ALU.min, op1=ALU.max), [("r", w)], [("r", w)])
                snl, csl = T_.sn[slot], T_.cs[slot]
                r0, r1 = T_.r[0], T_.r[1]
                P.op("act", lambda e: e.activation(out=snl[0:64, :], in_=r0[0:64, :], func=AF.Sin, scale=-1.0), [("r", 0)], [("sn", slot, 0)])
                P.op("act", lambda e: e.activation(out=snl[64:128, :], in_=r0[64:128, :], func=AF.Sin), [("r", 0)], [("sn", slot, 1)])
                P.op("act", lambda e: e.activation(out=csl, in_=r1, func=AF.Sin), [("r", 1)], [("cs", slot)])

        def rope(ps, pskey, tab, slot, u, v, outap, okey):
            cs, sn = tab.cs[slot], tab.sn[slot]
            P.op("dve", lambda e: e.tensor_tensor(out=u[0:64, :], in0=ps[64:128, :], in1=sn[0:64, :], op=ALU.mult),
                 [pskey, ("sn", slot, 0)], [(okey, "ul")])
            P.op("dve", lambda e: e.tensor_tensor(out=u[64:128, :], in0=ps[0:64, :], in1=sn[64:128, :], op=ALU.mult),
                 [pskey, ("sn", slot, 1)], [(okey, "uh")])
            P.op("dve", lambda e: e.tensor_tensor(out=v, in0=ps, in1=cs, op=ALU.mult), [pskey, ("cs", slot)], [(okey, "v")])
            P.op("dve", lambda e: e.tensor_tensor(out=outap, in0=v, in1=u, op=ALU.add),
                 [(okey, "ul"), (okey, "uh"), (okey, "v")], [okey])

        def mm_group(psap, pskey, pairs, extra_reads=()):
            n = len(pairs)
            for i, (l, r, rd) in enumerate(pairs):
                P.op("pe", lambda e, l=l, r=r, i=i: e.matmul(psap, lhsT=l, rhs=r, start=(i == 0), stop=(i == n - 1)),
                     list(rd) + list(extra_reads), [pskey])

        A.reset(PERSIST)
        stage = Stage(3)
        Wkv = A.bf16(16 * 2048)
        Wkv_v = Wkv.rearrange("p (c n) -> p c n", c=16)
        xbs = [A.bf16(16 * 512) for _ in range(2)]
        tab = RopeTab()
        ua = [A.f32(512) for _ in range(2)]
        va = [A.f32(512) for _ in range(2)]
        kr = [A.f32(512) for _ in range(2)]
        kb_all = A.bf16(8 * 512)
        kb_v = kb_all.rearrange("p (h t) -> p h t", h=8)
        vb_all = A.bf16(4 * 1024)
        vb_v = vb_all.rearrange("p (s e) -> p s e", s=4)
        kmean_v = kmean.rearrange("p (h n) -> p h n", h=8)

        wkeys = load_weights(stage, Wkv, w_in, 1024, 2048, "Wkv")
        Kscr_v = Kscr.rearrange("h p t -> p h t")

        load_xT(stage, xbs[0], xT_all, 0, ("xa", 0), engs=("pool",))
        tab.make(pos_all, 0, 0)
        psrr = [0]
        for tt in range(NTA):
            if tt + 1 < NTA:
                load_xT(stage, xbs[(tt + 1) % 2], xT_all, (tt + 1) * 512, ("xa", (tt + 1) % 2), engs=("pool",))
            xb = xbs[tt % 2].rearrange("p (c t) -> p c t", c=16)
            sl = tt % 2
            for h in range(8):
                b_ = psrr[0] % 4
                psrr[0] += 1
                ps = pbank[b_][:]
                mm_group(ps, ("ps", b_), [(Wkv_v[:, dc, h * 128:(h + 1) * 128], xb[:, dc, :],
                                          [("Wkv", dc, 0), ("xa", sl, dc // 4)]) for dc in range(16)])
                i2 = (tt * 8 + h) % 2
                rope(ps, ("ps", b_), tab, sl, ua[i2], va[i2], kr[i2], ("kr", i2))
                P.op("pool", lambda e, i2=i2, h=h: e.tensor_copy(out=kb_v[:, h, :], in_=kr[i2]), [("kr", i2)], [("kb", h)])
                P.op("dve", lambda e, i2=i2, h=h, tt=tt: e.tensor_reduce(
                    out=kmean_v[:, h, 2 * tt:2 * tt + 2], in_=kr[i2].rearrange("p (b k) -> p b k", b=2), axis=AX.X, op=ALU.add),
                    [("kr", i2)], [("kmean", h, tt)])
            P.dma("pool", Kscr_v[:, :, tt * 512:(tt + 1) * 512], kb_v, reads=[("kb", h) for h in range(8)],
                  writes=[("Kscr", tt)], semkey="kb")
            if tt + 1 < NTA:
                tab.make(pos_all, (tt + 1) * 512, (tt + 1) % 2)
            for sub in range(4):
                for half in range(2):
                    b_ = 4 + psrr[0] % 4
                    psrr[0] += 1
                    ps = pbank[b_][:]
                    mm_group(ps, ("ps", b_), [(xb[:, dc, sub * 128:(sub + 1) * 128], Wkv_v[:, dc, 1024 + half * 512:1024 + (half + 1) * 512],
                                              [("Wkv", dc, 0), ("xa", sl, dc // 4)]) for dc in range(16)])
                    P.op("act", lambda e, ps=ps, sub=sub, half=half: e.copy(out=vb_v[:, sub, half * 512:(half + 1) * 512], in_=ps),
                         [("ps", b_)], [("vb", sub, half)])
            P.dma("pool", Vscr[tt * 512:(tt + 1) * 512, :].rearrange("(s p) e -> p s e", p=128), vb_v,
                  reads=[("vb", s_, h_) for s_ in range(4) for h_ in range(2)], writes=[("Vscr", tt)], semkey="vb")
        P.op("dve", lambda e: e.tensor_scalar(out=kmean_bf, in0=kmean, scalar1=1.0 / 256.0, scalar2=None, op0=ALU.mult),
             [("kmean", h, tt) for h in range(8) for tt in range(NTA)], ["kmean_bf"])
        P.barrier()

        kmean_bf_v = kmean_bf.rearrange("p (h n) -> p h n", h=8)
        Qscr_v = Qscr.rearrange("h p t -> p h t")
        Gscr_v = Gscr.rearrange("h p t -> p h t")

        A.reset(PERSIST)
        stage = Stage(3)
        Wq = A.bf16(16 * 1024)
        Wq_v = Wq.rearrange("p (c n) -> p c n", c=16)
        xbs = [A.bf16(16 * 512) for _ in range(2)]
        tab = RopeTab()
        ua = [A.f32(512) for _ in range(2)]
        va = [A.f32(512) for _ in range(2)]
        qr = [A.f32(512) for _ in range(2)]
        qb_all = A.bf16(8 * 512)
        qb_v = qb_all.rearrange("p (h t) -> p h t", h=8)
        pastneg_s = A.f32(NTB * 4 * NB)
        notown_s = A.f32(NTB * 4 * NB)
        pastneg_v = pastneg_s.rearrange("p (t s n) -> p t s n", t=NTB, s=4)
        notown_v = notown_s.rearrange("p (t s n) -> p t s n", t=NTB, s=4)
        gm4 = [A.f32(4 * NB) for _ in range(2)]
        top84 = [A.f32(32) for _ in range(2)]
        thr4 = [A.f32(4) for _ in range(2)]
        lt4 = [A.f32(4 * NB) for _ in range(2)]
        rmask4 = [A.bf16(4 * NB) for _ in range(2)]
        rt_all = A.bf16(8 * 512)
        rt_v = rt_all.rearrange("p (h t) -> p h t", h=8)
        RTscr_v = RTscr.rearrange("h s t -> s h t")
        P.dma("sp", pastneg_s, pastneg.partition_broadcast(128), writes=["pastneg"], semkey="pastneg")
        P.dma("sp", notown_s, notown.partition_broadcast(128), writes=["notown"], semkey="notown")
        load_weights(stage, Wq, w_in, 0, 1024, "Wq")
        load_xT(stage, xbs[0], xT_own, 0, ("xo", 0))
        tab.make(pos_own, 0, 0)
        ptb = pbank[7].bitcast(BF16)
        for tt in range(NTB):
            if tt + 1 < NTB:
                load_xT(stage, xbs[(tt + 1) % 2], xT_own, (tt + 1) * 512, ("xo", (tt + 1) % 2))
            xb = xbs[tt % 2].rearrange("p (c t) -> p c t", c=16)
            sl = tt % 2
            for step in range(10):
                if step < 8:
                    h = step
                    b_ = psrr[0] % 4
                    psrr[0] += 1
                    ps = pbank[b_][:]
                    mm_group(ps, ("ps", b_), [(Wq_v[:, dc, h * 128:(h + 1) * 128], xb[:, dc, :],
                                              [("Wq", dc, 0), ("xo", sl, dc // 4)]) for dc in range(16)])
                    i2 = h % 2
                    rope(ps, ("ps", b_), tab, sl, ua[i2], va[i2], qr[i2], ("qr", i2))
                    P.op("act", lambda e, i2=i2, h=h: e.copy(out=qb_v[:, h, :], in_=qr[i2]), [("qr", i2)], [("qb", h)])
                if 1 <= step <= 8:
                    h = step - 1
                    g2 = h % 2
                    pg = pbank[4 + g2][:, 0:4 * NB]
                    for sub in range(4):
                        P.op("pe", lambda e, pg=pg, h=h, sub=sub: e.matmul(pg[:, sub * NB:(sub + 1) * NB], lhsT=qb_v[:, h, sub * 128:(sub + 1) * 128],
                                                                           rhs=kmean_bf_v[:, h, :], start=True, stop=True),
                             [("qb", h), "kmean_bf"], [("ps", 4 + g2)])
                    gmv = gm4[g2].rearrange("p (s n) -> p s n", s=4)
                    P.op("dve", lambda e, pg=pg, gmv=gmv, tt=tt: e.tensor_tensor(out=gmv, in0=pg.rearrange("p (s n) -> p s n", s=4),
                                                                                in1=pastneg_v[:, tt], op=ALU.add),
                         [("ps", 4 + g2), "pastneg"], [("gm", g2)])
                    for sub in range(4):
                        P.op("dve", lambda e, g2=g2, sub=sub: e.max(out=top84[g2][:, sub * 8:(sub + 1) * 8], in_=gm4[g2][:, sub * NB:(sub + 1) * NB]),
                             [("gm", g2)], [("top8", g2, sub)])
                    P.op("dve", lambda e, g2=g2: e.tensor_scalar(out=thr4[g2], in0=top84[g2].rearrange("p (s e) -> p s e", s=4)[:, :, 2],
                                                                 scalar1=-1e29, scalar2=None, op0=ALU.max),
                         [("top8", g2, s_) for s_ in range(4)], [("thr", g2)])
                    ltv = lt4[g2].rearrange("p (s n) -> p s n", s=4)
                    P.op("dve", lambda e, g2=g2, gmv=gmv, ltv=ltv: e.tensor_tensor(out=ltv, in0=gmv, in1=thr4[g2].unsqueeze(2).to_broadcast([128, 4, NB]),
                                                                                  op=ALU.is_lt), [("gm", g2), ("thr", g2)], [("lt", g2)])
                    P.op("dve", lambda e, g2=g2, ltv=ltv, tt=tt: e.tensor_tensor(out=rmask4[g2].rearrange("p (s n) -> p s n", s=4), in0=ltv,
                                                                                in1=notown_v[:, tt], op=ALU.mult),
                         [("lt", g2), "notown"], [("rmask", g2)])
                if 2 <= step <= 9:
                    h = step - 2
                    g2 = h % 2
                    for sub in range(4):
                        pt = ptb[0:NB, g2 * 512 + sub * 128:g2 * 512 + (sub + 1) * 128]
                        P.op("pe", lambda e, pt=pt, g2=g2, sub=sub: e.transpose(pt, rmask4[g2][:, sub * NB:(sub + 1) * NB], ident),
                             [("rmask", g2), "ident"], [("ptb", g2)])
                    P.op("act", lambda e, g2=g2, h=h: e.copy(out=rt_v[0:NB, h, :], in_=ptb[0:NB, g2 * 512:(g2 + 1) * 512]),
                         [("ptb", g2)], [("rt", h)])
            if tt + 1 < NTB:
                tab.make(pos_own, (tt + 1) * 512, (tt + 1) % 2)
            P.dma("pool", Qscr_v[:, :, tt * 512:(tt + 1) * 512], qb_v, reads=[("qb", h) for h in range(8)],
                  writes=[("Qscr", tt)], semkey="qb")
            P.dma("pool", RTscr_v[:, :, tt * 512:(tt + 1) * 512], rt_v[0:NB], reads=[("rt", h) for h in range(8)],
                  writes=[("RTscr", tt)], semkey="rt")
        P.barrier()

        A.reset(PERSIST)
        stage = Stage(3)
        Wg = A.bf16(16 * 1024)
        Wg_v = Wg.rearrange("p (c n) -> p c n", c=16)
        xbs = [A.bf16(16 * 512) for _ in range(2)]
        gb_all = A.bf16(8 * 512)
        gb_v = gb_all.rearrange("p (h t) -> p h t", h=8)
        load_weights(stage, Wg, w_in, 3072, 1024, "Wg")
        load_xT(stage, xbs[0], xT_own, 0, ("xo", 0))
        for tt in range(NTB):
            if tt + 1 < NTB:
                load_xT(stage, xbs[(tt + 1) % 2], xT_own, (tt + 1) * 512, ("xo", (tt + 1) % 2))
            xb = xbs[tt % 2].rearrange("p (c t) -> p c t", c=16)
            sl = tt % 2
            for h in range(8):
                b_ = psrr[0] % 4
                psrr[0] += 1
                ps = pbank[b_][:]
                mm_group(ps, ("ps", b_), [(Wg_v[:, dc, h * 128:(h + 1) * 128], xb[:, dc, :],
                                          [("Wg", dc, 0), ("xo", sl, dc // 4)]) for dc in range(16)])
                P.op("act", lambda e, ps=ps, h=h: e.activation(out=gb_v[:, h, :], in_=ps, func=AF.Silu), [("ps", b_)], [("gb", h)])
            P.dma("pool", Gscr_v[:, :, tt * 512:(tt + 1) * 512], gb_v, reads=[("gb", h) for h in range(8)],
                  writes=[("Gscr", tt)], semkey="gb")
        P.barrier()

        A.reset(PERSIST)
        stage = Stage(3)
        W3 = A.bf16(16 * 1536)
        W3_v = W3.rearrange("p (c n) -> p c n", c=16)
        xbs = [A.bf16(16 * 512) for _ in range(2)]
        glng_s = A.f32(512)
        glnb_s = A.f32(512)
        bs_s = A.f32(512)
        ws_f = A.f32(512)
        ws_b = A.bf16(512)
        ws_bv = ws_b.rearrange("p (g t) -> p g t", g=4)
        ug = A.f32(4 * 512)
        ug_v = ug.rearrange("p (g t) -> p g t", g=4)
        sg = A.f32(4 * 512)
        sg_v = sg.rearrange("p (g t) -> p g t", g=4)
        vg = [A.f32(512) for _ in range(2)]
        junk = A.f32(512)
        s1 = [A.f32(1) for _ in range(2)]
        s2 = [A.f32(1) for _ in range(2)]
        mu = [A.f32(1) for _ in range(2)]
        var = [A.f32(1) for _ in range(2)]
        rstd = [A.f32(1) for _ in range(2)]
        vln = [A.bf16(512) for _ in range(2)]
        mx = [A.f32(512) for _ in range(2)]
        ygm = A.bf16(4 * 512)
        ygm_v = ygm.rearrange("p (g t) -> p g t", g=4)
        P.dma("sp", glng_s, glng.partition_broadcast(128), writes=["glng"], semkey="glng")
        P.dma("sp", glnb_s, glnb.partition_broadcast(128), writes=["glnb"], semkey="glnb")
        P.dma("sp", bs_s, bs.partition_broadcast(128), writes=["bs"], semkey="bs")
        P.dma("sp", ws_f, wsT, writes=["ws_f"], semkey="ws_f")
        P.op("pool", lambda e: e.affine_select(out=ws_f.rearrange("p (g t) -> p g t", g=4), in_=ws_f.rearrange("p (g t) -> p g t", g=4),
                                               pattern=[[0, 4], [1, 128]], compare_op=ALU.is_ge, fill=0.0, base=0,
                                               channel_multiplier=-1), ["ws_f"], ["ws_f"])
        P.op("pool", lambda e: e.tensor_copy(out=ws_b, in_=ws_f), ["ws_f"], ["ws_b"])
        load_weights(stage, W3, w_in, 4096, 1536, "W3")
        load_xT(stage, xbs[0], xT_own, 0, ("xo", 0))
        Yscr_g = Yscr[1024:1536, :].rearrange("(g p) t -> p g t", p=128)
        for tt in range(NTB):
            if tt + 1 < NTB:
                load_xT(stage, xbs[(tt + 1) % 2], xT_own, (tt + 1) * 512, ("xo", (tt + 1) % 2))
            xb = xbs[tt % 2].rearrange("p (c t) -> p c t", c=16)
            sl = tt % 2
            for g in range(4):
                for which, c0, func, dstv, dk in ((0, 0, AF.Gelu, ug_v, "ug"), (1, 1024, AF.Silu, sg_v, "sg")):
                    b_ = psrr[0] % 4
                    psrr[0] += 1
                    ps = pbank[b_][:]
                    mm_group(ps, ("ps", b_), [(W3_v[:, dc, c0 + g * 128:c0 + (g + 1) * 128], xb[:, dc, :],
                                              [("W3", dc, 0), ("xo", sl, dc // 4)]) for dc in range(16)])
                    P.op("act", lambda e, ps=ps, g=g, func=func, dstv=dstv: e.activation(out=dstv[:, g, :], in_=ps, func=func),
                         [("ps", b_)], [(dk, g)])
            mix_pend = None
            for sub in range(4):
                k2 = sub % 2
                b_ = psrr[0] % 4
                psrr[0] += 1
                ps = pbank[b_][:]
                mm_group(ps, ("ps", b_), [(xb[:, dc, sub * 128:(sub + 1) * 128], W3_v[:, dc, 512:1024],
                                          [("W3", dc, 0), ("xo", sl, dc // 4)]) for dc in range(16)])
                P.op("act", lambda e, ps=ps, k2=k2: e.activation(out=vg[k2], in_=ps, func=AF.Gelu), [("ps", b_)], [("vg", k2)])
                if mix_pend is not None:
                    mix_pend()
                P.op("dve", lambda e, k2=k2: e.tensor_reduce(out=s1[k2], in_=vg[k2], axis=AX.X, op=ALU.add), [("vg", k2)], [("s1", k2)])
                P.op("dve", lambda e, k2=k2: e.scalar_tensor_tensor(out=junk, in0=vg[k2], scalar=1.0, in1=vg[k2], op0=ALU.mult,
                                                                    op1=ALU.mult, accum_out=s2[k2]), [("vg", k2)], ["junk", ("s2", k2)])
                P.op("dve", lambda e, k2=k2: e.tensor_scalar(out=mu[k2], in0=s1[k2], scalar1=1.0 / 512.0, scalar2=None, op0=ALU.mult),
                     [("s1", k2)], [("mu", k2)])
                P.op("dve", lambda e, k2=k2: e.tensor_tensor(out=var[k2], in0=mu[k2], in1=mu[k2], op=ALU.mult), [("mu", k2)], [("var", k2)])
                P.op("dve", lambda e, k2=k2: e.scalar_tensor_tensor(out=var[k2], in0=s2[k2], scalar=1.0 / 512.0, in1=var[k2],
                                                                    op0=ALU.mult, op1=ALU.subtract), [("s2", k2), ("var", k2)], [("var", k2)])
                P.op("act", lambda e, k2=k2: e.activation(out=rstd[k2], in_=var[k2], func=AF.Sqrt, bias=eps_s[:, 0:1], scale=1.0),
                     [("var", k2), "eps"], [("rstd", k2)])
                P.op("dve", lambda e, k2=k2: e.reciprocal(out=rstd[k2], in_=rstd[k2]), [("rstd", k2)], [("rstd", k2)])
                P.op("dve", lambda e, k2=k2: e.tensor_scalar(out=vg[k2], in0=vg[k2], scalar1=mu[k2][:, 0:1], scalar2=rstd[k2][:, 0:1],
                                                             op0=ALU.subtract, op1=ALU.mult), [("vg", k2), ("mu", k2), ("rstd", k2)], [("vg", k2)])
                P.op("pool", lambda e, k2=k2: e.tensor_tensor(out=vg[k2], in0=vg[k2], in1=glng_s, op=ALU.mult), [("vg", k2), "glng"], [("vg", k2)])
                P.op("pool", lambda e, k2=k2: e.tensor_tensor(out=vln[k2], in0=vg[k2], in1=glnb_s, op=ALU.add), [("vg", k2), "glnb"], [("vln", k2)])
                def mix(sub=sub, k2=k2):
                    bm = 4 + (psrr[0] % 2)
                    psm = pbank[bm][:]
                    for g in range(4):
                        P.op("pe", lambda e, psm=psm, g=g, k2=k2: e.matmul(psm[:, g * 128:(g + 1) * 128], lhsT=vln[k2][:, g * 128:(g + 1) * 128],
                                                                           rhs=ws_bv[:, g, :], start=True, stop=True),
                             [("vln", k2), "ws_b"], [("ps", bm)])
                    P.op("dve", lambda e, psm=psm, k2=k2: e.tensor_tensor(out=mx[k2], in0=psm, in1=bs_s, op=ALU.add), [("ps", bm), "bs"], [("mx", k2)])
                    mxv = mx[k2].rearrange("p (g t) -> p g t", g=4)
                    P.op("dve", lambda e, mxv=mxv, sub=sub: e.tensor_tensor(out=mxv, in0=mxv, in1=ug_v[:, :, sub * 128:(sub + 1) * 128], op=ALU.mult),
                         [("mx", k2)] + [("ug", g) for g in range(4)], [("mx", k2)])
                    P.op("pool", lambda e, mxv=mxv, sub=sub: e.tensor_tensor(out=ygm_v[:, :, sub * 128:(sub + 1) * 128], in0=mxv,
                                                                             in1=sg_v[:, :, sub * 128:(sub + 1) * 128], op=ALU.mult),
                         [("mx", k2)] + [("sg", g) for g in range(4)], [("ygm", sub)])
                mix_pend = mix
            mix_pend()
            P.dma("pool", Yscr_g[:, :, tt * 512:(tt + 1) * 512], ygm_v, reads=[("ygm", s_) for s_ in range(4)],
                  writes=[("Yscr_g", tt)], semkey="ygm")
        P.barrier()

        A.reset(PERSIST)
        stage = Stage(3)
        W4 = A.bf16(16 * 1024)
        W4_v = W4.rearrange("p (c n) -> p c n", c=16)
        Wm = A.bf16(16 * 1024)
        Wm_v = Wm.rearrange("p (c n) -> p c n", c=16)
        mT_f = A.f32(16 * 256)
        mT_b = A.bf16(16 * 256)
        mT_v = mT_b.rearrange("p (c m) -> p c m", c=16)
        memKT = A.bf16(4 * 256)
        memKT_v = memKT.rearrange("p (h m) -> p h m", h=4)
        memV = A.bf16(2 * 512)
        memV_v = memV.rearrange("p (c e) -> p c e", c=2)
        xbs = [A.bf16(16 * 512) for _ in range(2)]
        qm = [A.bf16(512) for _ in range(2)]
        sgm = [A.f32(512) for _ in range(2)]
        pT = [A.bf16(512) for _ in range(4)]
        rec = [A.f32(512) for _ in range(2)]
        yt = [A.f32(512) for _ in range(2)]
        yme = A.bf16(4 * 512)
        yme_v = yme.rearrange("p (h t) -> p h t", h=4)
        P.dma("sp", mT_f.rearrange("p (c m) -> p c m", c=16), memT.rearrange("(c p) m -> p c m", p=128), writes=["mT_f"], semkey="mT_f")
        P.op("dve", lambda e: e.tensor_copy(out=mT_b, in_=mT_f), ["mT_f"], ["mT_b"])
        wmk = load_weights(stage, Wm, w_mkv, 0, 1024, "Wm")
        load_weights(stage, W4, w_in, 5632, 1024, "W4")
        for hh in range(4):
            b_ = psrr[0] % 4
            psrr[0] += 1
            ps = pbank[b_][:, 0:256]
            mm_group(ps, ("ps", b_), [(Wm_v[:, dc, hh * 128:(hh + 1) * 128], mT_v[:, dc, :], [("Wm", dc, 0), "mT_b"]) for dc in range(16)])
            P.op("act", lambda e, ps=ps, hh=hh: e.copy(out=memKT_v[:, hh, :], in_=ps), [("ps", b_)], [("memKT", hh)])
        for mc in range(2):
            b_ = psrr[0] % 4
            psrr[0] += 1
            ps = pbank[b_][:]
            mm_group(ps, ("ps", b_), [(mT_v[:, dc, mc * 128:(mc + 1) * 128], Wm_v[:, dc, 512:1024], [("Wm", dc, 0), "mT_b"]) for dc in range(16)])
            P.op("act", lambda e, ps=ps, mc=mc: e.copy(out=memV_v[:, mc, :], in_=ps), [("ps", b_)], [("memV", mc)])
        load_xT(stage, xbs[0], xT_own, 0, ("xo", 0))
        Yscr_m = Yscr[1536:2048, :].rearrange("(h p) t -> p h t", p=128)
        for tt in range(NTB):
            if tt + 1 < NTB:
                load_xT(stage, xbs[(tt + 1) % 2], xT_own, (tt + 1) * 512, ("xo", (tt + 1) % 2))
            xb = xbs[tt % 2].rearrange("p (c t) -> p c t", c=16)
            sl = tt % 2
            for hh in range(4):
                k2 = hh % 2
                b_ = psrr[0] % 2
                psrr[0] += 1
                ps = pbank[b_][:]
                mm_group(ps, ("ps", b_), [(W4_v[:, dc, hh * 128:(hh + 1) * 128], xb[:, dc, :],
                                          [("W4", dc, 0), ("xo", sl, dc // 4)]) for dc in range(16)])
                P.op("dve", lambda e, ps=ps, k2=k2: e.tensor_copy(out=qm[k2], in_=ps), [("ps", b_)], [("qm", k2)])
                bg = 2 + k2
                psg = pbank[bg][:]
                mm_group(psg, ("ps", bg), [(W4_v[:, dc, 512 + hh * 128:512 + (hh + 1) * 128], xb[:, dc, :],
                                           [("W4", dc, 0), ("xo", sl, dc // 4)]) for dc in range(16)])
                P.op("act", lambda e, psg=psg, k2=k2: e.activation(out=sgm[k2], in_=psg, func=AF.Silu), [("ps", bg)], [("sgm", k2)])
                for mc in range(2):
                    bsn = 4 + mc
                    pss = pbank[bsn][:]
                    P.op("pe", lambda e, pss=pss, hh=hh, mc=mc, k2=k2: e.matmul(pss, lhsT=memKT_v[:, hh, mc * 128:(mc + 1) * 128], rhs=qm[k2],
                                                                               start=True, stop=True), [("memKT", hh), ("qm", k2)], [("ps", bsn)])
                    pi = k2 * 2 + mc
                    P.op("act", lambda e, pss=pss, pi=pi: e.activation(out=pT[pi], in_=pss, func=AF.Exp, scale=SCALE), [("ps", bsn)], [("pT", pi)])
                pso = pbank[6][:]
                psd = pbank[7][:]
                mm_group(pso, ("ps", 6), [(memV_v[:, mc, hh * 128:(hh + 1) * 128], pT[k2 * 2 + mc], [("memV", mc), ("pT", k2 * 2 + mc)]) for mc in range(2)])
                mm_group(psd, ("ps", 7), [(ones_bf, pT[k2 * 2 + mc], ["ones_bf", ("pT", k2 * 2 + mc)]) for mc in range(2)])
                P.op("dve", lambda e, psd=psd, k2=k2: e.reciprocal(out=rec[k2], in_=psd), [("ps", 7)], [("rec", k2)])
                P.op("dve", lambda e, pso=pso, k2=k2: e.tensor_tensor(out=yt[k2], in0=pso, in1=rec[k2], op=ALU.mult), [("ps", 6), ("rec", k2)], [("yt", k2)])
                P.op("pool", lambda e, k2=k2, hh=hh: e.tensor_tensor(out=yme_v[:, hh, :], in0=yt[k2], in1=sgm[k2], op=ALU.mult),
                     [("yt", k2), ("sgm", k2)], [("yme", hh)])
            P.dma("pool", Yscr_m[:, :, tt * 512:(tt + 1) * 512], yme_v, reads=[("yme", h_) for h_ in range(4)],
                  writes=[("Yscr_m", tt)], semkey="yme")
        P.barrier()

        A.reset(PERSIST)
        Kh = A.bf16(NB * 256)
        Kh_v = Kh.rearrange("p (s k) -> p s k", s=NB)
        Vh = A.bf16(NB * 256)
        Vh_v = Vh.rearrange("p (s c d) -> p s c d", s=NB, c=2)
        Qh = [A.bf16(NOWN) for _ in range(2)]
        Gh = [A.bf16(NOWN) for _ in range(2)]
        RTh = [A.bf16(NOWN) for _ in range(2)]
        Esel = A.bf16(NB * 128)
        Esel_v = Esel.rearrange("p (s k) -> p s k", s=NB)
        m2_f = [A.f32(2048) for _ in range(1)]
        m2_b = A.bf16(8 * 2 * 512)
        m2_v = m2_b.rearrange("p (r c q) -> p r c q", r=8, c=2)
        PT2 = [A.bf16(1024) for _ in range(3)]
        accD = A.f32(1024)
        recm = A.f32(512)
        ytm = A.f32(512)
        ymo = [A.bf16(512) for _ in range(2)]
        P.op("pool", lambda e: e.memset(Esel, 1.0), [], ["Esel"])
        P.op("pool", lambda e: e.affine_select(out=Esel_v, in_=Esel_v, pattern=[[-1, NB], [0, 128]], compare_op=ALU.is_equal,
                                               fill=0.0, base=0, channel_multiplier=1), ["Esel"], ["Esel"])
        for hs_ in range(2):
            P.op("pool", lambda e, hs_=hs_: e.memset(RTh[hs_], 0.0), [], [("RTh", hs_)])
        for i4 in range(4):
            P.dma("sp", m2_f[0], m2d[:, i4 * 2048:(i4 + 1) * 2048], writes=["m2_f"], semkey="m2_f")
            P.op("dve", lambda e, i4=i4: e.tensor_copy(out=m2_b[:, i4 * 2048:(i4 + 1) * 2048], in_=m2_f[0]), ["m2_f"], [("m2_b", i4)])
        m2keys = [("m2_b", i4) for i4 in range(4)]
        Vscr_v = Vscr.rearrange("(s c k) (h d) -> h k s c d", c=2, k=128, d=128)
        Kscr_s = Kscr.rearrange("h p (s k) -> h p s k", k=256)
        Yscr_h = Yscr[0:1024, :].rearrange("(h p) t -> h p t", p=128)
        nstep = 0
        for h in range(8):
            hs = h % 2
            P.dma("sp", Qh[hs], Qscr[h], reads=[("Qscr", tt) for tt in range(NTB)], writes=[("Qh", hs)], semkey=("Qh", hs))
            P.dma("sp", Gh[hs], Gscr[h], reads=[("Gscr", tt) for tt in range(NTB)], writes=[("Gh", hs)], semkey=("Gh", hs))
            P.dma("sp", RTh[hs][0:NB], RTscr[h], reads=[("RTscr", tt) for tt in range(NTB)], writes=[("RTh", hs)], semkey=("RTh", hs))
            for g in range(NB // 8 - 1, -1, -1):
                P.op("sp", lambda e, g=g, h=h: [e.dma_start(out=Kh_v[:, g * 8:(g + 1) * 8, :], in_=Kscr_s[h, :, g * 8:(g + 1) * 8, :]),
                                                e.dma_start(out=Vh_v[:, g * 8:(g + 1) * 8], in_=Vscr_v[h, :, g * 8:(g + 1) * 8])],
                     reads=[("Kscr", t_) for t_ in range(g * 4, g * 4 + 4)] + [("Vscr", t_) for t_ in range(g * 4, g * 4 + 4)],
                     writes=[("Kh", g), ("Vh", g)], semkey=("KV", g), ndma=2)
            for p in range(NPAIR - 1, -1, -1):
                q0 = p * 512
                nsl = 8 * p + 8
                ob = 6
                psO = pbank[ob][:]
                nstep += 1
                used = {"accD": False, "accP": False}

                def qk(s, j):
                    p3 = j % 3
                    for c in range(2):
                        bS = p3 * 2 + c
                        psS = pbank[bS][:]
                        terms = [(Kh_v[:, s, c * 128:(c + 1) * 128], Qh[hs][:, q0:q0 + 512], [("Kh", s // 8), ("Qh", hs)]),
                                 (Esel_v[:, s, :], RTh[hs][:, q0:q0 + 512], ["Esel", ("RTh", hs)])]
                        if s >= 8 * p:
                            terms.append((ident, m2_v[:, s - 8 * p, c, :], ["ident"] + m2keys))
                        mm_group(psS, ("ps", bS), terms)
                    bS0 = p3 * 2
                    P.op("act", lambda e, bS0=bS0, p3=p3: e.activation(out=PT2[p3], in_=ps_all[:, bS0 * 512:(bS0 + 2) * 512], func=AF.Exp, scale=SCALE),
                         [("ps", bS0), ("ps", bS0 + 1)], [("PT", p3)])
                    if not used["accD"]:
                        used["accD"] = True
                        P.op("dve", lambda e, p3=p3: e.tensor_copy(out=accD, in_=PT2[p3]), [("PT", p3)], ["accD"])
                    else:
                        P.op("dve", lambda e, p3=p3: e.tensor_tensor(out=accD, in0=accD, in1=PT2[p3], op=ALU.add), [("PT", p3), "accD"], ["accD"])

                def pv(s, j):
                    p3 = j % 3
                    for c in range(2):
                        P.op("pe", lambda e, c=c, p3=p3, s=s, j=j: e.matmul(psO, lhsT=Vh_v[:, s, c, :], rhs=PT2[p3][:, c * 512:(c + 1) * 512],
                                                                       start=(j == 0 and c == 0), stop=(j == nsl - 1 and c == 1)),
                             [("Vh", s // 8), ("PT", p3)], [("ps", ob)])

                order = list(range(nsl - 1, -1, -1))
                for j in range(nsl + 2):
                    if j < nsl:
                        qk(order[j], j)
                    if j >= 2:
                        pv(order[j - 2], j - 2)
                psD = pbank[7][:]
                mm_group(psD, ("ps", 7), [(ones_f, accD[:, 0:512], ["ones_f", "accD"]), (ones_f, accD[:, 512:1024], ["ones_f", "accD"])])
                P.op("dve", lambda e, psD=psD: e.reciprocal(out=recm, in_=psD), [("ps", 7)], ["recm"])
                P.op("dve", lambda e, psO=psO: e.tensor_tensor(out=ytm, in0=psO, in1=recm, op=ALU.mult), [("ps", ob), "recm"], ["ytm"])
                y2 = nstep % 2
                P.op("pool", lambda e, y2=y2, hs=hs, q0=q0: e.tensor_tensor(out=ymo[y2], in0=ytm, in1=Gh[hs][:, q0:q0 + 512], op=ALU.mult),
                     ["ytm", ("Gh", hs)], [("ymo", y2)])
                P.dma("pool", Yscr_h[h, :, q0:q0 + 512], ymo[y2], reads=[("ymo", y2)], writes=[("Yscr_h", h, p)], semkey=("ymo", y2))
        P.barrier()

        A.reset(PERSIST)
        stage = Stage(3)
        Wo = A.bf16(16 * 2048)
        Wo_v = Wo.rearrange("p (c n) -> p c n", c=16)
        lng_s = A.f32(2048)
        lnb_s = A.f32(2048)
        yT = [A.bf16(16 * 128) for _ in range(2)]
        xo = [A.f32(2048) for _ in range(2)]
        z = [A.f32(2048) for _ in range(2)]
        junk2 = A.f32(2048)
        d1 = [A.f32(1) for _ in range(2)]
        d2 = [A.f32(1) for _ in range(2)]
        dmu = [A.f32(1) for _ in range(2)]
        dvar = [A.f32(1) for _ in range(2)]
        drs = [A.f32(1) for _ in range(2)]
        P.dma("sp", lng_s, lng.partition_broadcast(128), writes=["lng"], semkey="lng")
        P.dma("sp", lnb_s, lnb.partition_broadcast(128), writes=["lnb"], semkey="lnb")
        load_weights(stage, Wo, w_out, 0, 2048, "Wo")
        Yscr_c = Yscr.rearrange("(c p) t -> p c t", p=128)
        NTD = NOWN // 128
        def d_loads(t):
            k2 = t % 2
            P.dma("sp", yT[k2].rearrange("p (c t) -> p c t", c=16), Yscr_c[:, :, t * 128:(t + 1) * 128], writes=[("yT", k2)], semkey=("yT", k2))
            P.dma("sp", xo[k2], x_own[t * 128:(t + 1) * 128, :], writes=[("xo_d", k2)], semkey=("xo_d", k2))

        d_loads(0)
        for t in range(NTD):
            k2 = t % 2
            if t + 1 < NTD:
                d_loads(t + 1)
            yv = yT[k2].rearrange("p (c t) -> p c t", c=16)
            for q4 in range(4):
                b_ = (t % 2) * 4 + q4
                ps = pbank[b_][:]
                mm_group(ps, ("ps", b_), [(yv[:, ec, :], Wo_v[:, ec, q4 * 512:(q4 + 1) * 512], [("yT", k2), ("Wo", ec, 0)]) for ec in range(16)])
                P.op("dve", lambda e, ps=ps, k2=k2, q4=q4: e.scalar_tensor_tensor(out=z[k2][:, q4 * 512:(q4 + 1) * 512], in0=xo[k2][:, q4 * 512:(q4 + 1) * 512],
                                                                                  scalar=ALPHA, in1=ps, op0=ALU.mult, op1=ALU.add),
                     [("ps", b_), ("xo_d", k2)], [("z", k2, q4)])
            zk = [("z", k2, q4) for q4 in range(4)]
            P.op("dve", lambda e, k2=k2: e.tensor_reduce(out=d1[k2], in_=z[k2], axis=AX.X, op=ALU.add), zk, [("d1", k2)])
            P.op("dve", lambda e, k2=k2: e.scalar_tensor_tensor(out=junk2, in0=z[k2], scalar=1.0, in1=z[k2], op0=ALU.mult, op1=ALU.mult,
                                                                accum_out=d2[k2]), zk, ["junk2", ("d2", k2)])
            P.op("dve", lambda e, k2=k2: e.tensor_scalar(out=dmu[k2], in0=d1[k2], scalar1=1.0 / 2048.0, scalar2=None, op0=ALU.mult), [("d1", k2)], [("dmu", k2)])
            P.op("dve", lambda e, k2=k2: e.tensor_tensor(out=dvar[k2], in0=dmu[k2], in1=dmu[k2], op=ALU.mult), [("dmu", k2)], [("dvar", k2)])
            P.op("dve", lambda e, k2=k2: e.scalar_tensor_tensor(out=dvar[k2], in0=d2[k2], scalar=1.0 / 2048.0, in1=dvar[k2], op0=ALU.mult,
                                                                op1=ALU.subtract), [("d2", k2), ("dvar", k2)], [("dvar", k2)])
            P.op("act", lambda e, k2=k2: e.activation(out=drs[k2], in_=dvar[k2], func=AF.Sqrt, bias=eps_s[:, 0:1], scale=1.0), [("dvar", k2), "eps"], [("drs", k2)])
            P.op("dve", lambda e, k2=k2: e.reciprocal(out=drs[k2], in_=drs[k2]), [("drs", k2)], [("drs", k2)])
            P.op("dve", lambda e, k2=k2: e.tensor_scalar(out=z[k2], in0=z[k2], scalar1=dmu[k2][:, 0:1], scalar2=drs[k2][:, 0:1], op0=ALU.subtract,
                                                         op1=ALU.mult), zk + [("dmu", k2), ("drs", k2)], zk)
            P.op("pool", lambda e, k2=k2: e.tensor_tensor(out=z[k2], in0=z[k2], in1=lng_s, op=ALU.mult), zk + ["lng"], zk)
            P.op("pool", lambda e, k2=k2: e.tensor_tensor(out=z[k2], in0=z[k2], in1=lnb_s, op=ALU.add), zk + ["lnb"], zk)
            P.dma("pool", out[t * 128:(t + 1) * 128, :], z[k2], reads=zk, writes=[("out", t)], semkey=("zo", k2))
        P.op("sp", None, reads=[("out", t) for t in range(NTD)])
        P.barrier()
        P.emit()
    return nc


def make_in_maps(x, mem, positions, w_in, w_mem_kv, gmlp_ln_g, gmlp_ln_b, gmlp_w_s, gmlp_b_s, w_out, ln_g, ln_b):
    B, T, D = x.shape
    NB = T // 256
    NI = NB // 4
    f32 = np.float32
    x = np.asarray(x, f32)
    mem = np.asarray(mem, f32)
    positions = np.asarray(positions).astype(np.int32)
    w_in0 = np.ascontiguousarray(np.asarray(w_in, f32)[0])
    w_out0 = np.ascontiguousarray(np.asarray(w_out, f32)[0])
    w_mkv0 = np.ascontiguousarray(np.asarray(w_mem_kv, f32)[0])
    ws = np.asarray(gmlp_w_s, f32)[0]
    wsT = np.ascontiguousarray(ws.transpose(2, 0, 1)).reshape(128, 512)
    bs = np.ascontiguousarray(np.asarray(gmlp_b_s, f32)[0].reshape(1, 512))
    half = 64
    inv = (10000.0 ** (-np.arange(half, dtype=np.float32) / half)).astype(f32)
    invf = np.concatenate([inv, inv]).reshape(128, 1).astype(f32)
    in_maps = []
    idx_list = []
    xT_cache = {}
    for c in range(8):
        b, jc = divmod(c, 4)
        own_blocks = [4 * i + jc for i in range(NI)]
        own_idx = np.concatenate([np.arange(n * 256, (n + 1) * 256) for n in own_blocks])
        idx_list.append((b, own_idx))
        if b not in xT_cache:
            xT_cache[b] = np.ascontiguousarray(x[b].T)
        x_own = np.ascontiguousarray(x[b][own_idx])
        pastneg = np.zeros((NI // 2, 4, NB), f32)
        notown = np.full((NI // 2, 4, NB), NEG, f32)
        for i, n in enumerate(own_blocks):
            for sb_ in range(2):
                pastneg[i // 2, (i % 2) * 2 + sb_, n:] = -1e30
                notown[i // 2, (i % 2) * 2 + sb_, n] = 0.0
        m2d = np.zeros((128, 8, 2, 512), f32)
        kk = np.arange(128)[:, None]
        qq = np.arange(256)[None, :]
        for r in range(8):
            for cc in range(2):
                for hf in range(2):
                    own_r = hf * 4 + jc
                    blk = m2d[:, r, cc, hf * 256:(hf + 1) * 256]
                    if r > own_r:
                        blk[:] = NEG
                    elif r == own_r:
                        blk[:] = np.where((cc * 128 + kk) > qq, NEG, 0.0)
        in_maps.append(dict(
            xT_all=xT_cache[b], xT_own=np.ascontiguousarray(x_own.T), x_own=x_own,
            pos_all=np.ascontiguousarray(positions[b][None, :]), pos_own=np.ascontiguousarray(positions[b][own_idx][None, :]),
            w_in=w_in0, w_out=w_out0, w_mkv=w_mkv0, memT=np.ascontiguousarray(mem[b].T),
            glng=np.asarray(gmlp_ln_g, f32).reshape(1, 512), glnb=np.asarray(gmlp_ln_b, f32).reshape(1, 512),
            wsT=wsT, bs=bs, lng=np.asarray(ln_g, f32).reshape(1, 2048), lnb=np.asarray(ln_b, f32).reshape(1, 2048),
            invf=invf, pastneg=pastneg.reshape(1, -1), notown=notown.reshape(1, -1), m2d=m2d.reshape(128, -1)))
    return in_maps, idx_list


_NC_CACHE = {}


def kernel(x, mem, positions, w_in, w_mem_kv, gmlp_ln_g, gmlp_ln_b, gmlp_w_s, gmlp_b_s, w_out, ln_g, ln_b):
    x = np.asarray(x)
    B, T, D = x.shape
    in_maps, idx_list = make_in_maps(x, mem, positions, w_in, w_mem_kv, gmlp_ln_g, gmlp_ln_b, gmlp_w_s, gmlp_b_s, w_out, ln_g, ln_b)
    if T not in _NC_CACHE:
        _NC_CACHE[T] = build(T)
    nc = _NC_CACHE[T]
    res = run_bass_kernel_spmd(nc, in_maps, core_ids=list(range(8)))
    outp = np.empty((B, T, D), np.float32)
    for c in range(8):
        b, own_idx = idx_list[c]
        outp[b, own_idx] = res.results[c]["out"]
    return outp
```

```python
import contextlib
import numpy as np
import concourse.bass as bass
import concourse.mybir as mybir
from concourse.bass_utils import run_bass_kernel_spmd

F32 = mybir.dt.float32
BF16 = mybir.dt.bfloat16
I32 = mybir.dt.int32
ALU = mybir.AluOpType
AF = mybir.ActivationFunctionType
AX = mybir.AxisListType

NEG = -30000.0
SCALE = 128.0 ** -0.5
ALPHA = 2.0 ** 0.25
LN_EPS = 1e-5
MAGIC = 12582912.0
TWO_PI = 2.0 * np.pi
C1 = 6.28125
C2 = float(TWO_PI - C1)
ARENA_WORDS = 49 * 1024


class Op:
    __slots__ = ("eng", "fn", "reads", "writes", "semkey", "waits", "needs_inc", "val", "is_dma", "extra", "ndma")

    def __init__(self, eng, fn, reads, writes, semkey, ndma=1):
        self.ndma = ndma
        self.eng = eng
        self.fn = fn
        self.reads = reads
        self.writes = writes
        self.semkey = semkey
        self.is_dma = semkey is not None
        self.waits = []
        self.needs_inc = False
        self.val = None
        self.extra = ()


class Prog:
    ENGS = ("pe", "act", "dve", "pool", "sp")

    def __init__(self, nc):
        self.nc = nc
        self.ops = []
        self.last_eng = {}
        self.last_dma = {}

    def op(self, eng, fn, reads=(), writes=(), semkey=None, ndma=1):
        o = Op(eng, fn, tuple(reads), tuple(writes), semkey, ndma)
        self.ops.append(o)
        if semkey is None:
            if fn is not None:
                self.last_eng[eng] = o
        else:
            self.last_dma[semkey] = o
        return o

    def dma(self, q, out, in_, reads=(), writes=(), semkey=None):
        assert semkey is not None
        return self.op(q, lambda e: e.dma_start(out=out, in_=in_), reads, writes, semkey)

    def barrier(self):
        pend = list(self.last_eng.values()) + list(self.last_dma.values())
        for eng in self.ENGS:
            o = self.op(eng, None)
            o.extra = tuple(pend)

    def analyze(self):
        last_writer = {}
        readers = {}
        for o in self.ops:
            deps = list(o.extra)
            for k in o.reads:
                w = last_writer.get(k)
                if w is not None:
                    deps.append(w)
            for k in o.writes:
                w = last_writer.get(k)
                if w is not None:
                    deps.append(w)
                deps.extend(readers.get(k, {}).values())
            for k in o.reads:
                rk = readers.setdefault(k, {})
                rk[(o.eng, id(o)) if o.is_dma else o.eng] = o
            for k in o.writes:
                last_writer[k] = o
                readers[k] = {}
            seen = set()
            for p in deps:
                if p is o or id(p) in seen:
                    continue
                seen.add(id(p))
                if p.eng == "pe" and o.eng == "pe" and not p.is_dma and not o.is_dma:
                    continue
                o.waits.append(p)
                p.needs_inc = True
        cnt = {}
        for o in self.ops:
            if o.is_dma:
                key = ("dma", o.semkey)
                cnt[key] = cnt.get(key, 0) + 16 * o.ndma
                o.val = (key, cnt[key])
            elif o.needs_inc:
                key = ("eng", o.eng)
                cnt[key] = cnt.get(key, 0) + 1
                o.val = (key, cnt[key])
        self.semkeys = sorted(cnt.keys(), key=str)

    def emit(self):
        nc = self.nc
        self.analyze()
        with contextlib.ExitStack() as st:
            sems = {}
            for i, k in enumerate(self.semkeys):
                sems[k] = st.enter_context(nc.semaphore("s%d" % i))
            block = st.enter_context(nc.Block())
            streams = {e: [] for e in self.ENGS}
            for o in self.ops:
                streams[o.eng].append(o)

            def run_stream(name):
                def body(e):
                    known = {}
                    for o in streams[name]:
                        need = {}
                        for p in o.waits:
                            k, v = p.val
                            if v > need.get(k, 0):
                                need[k] = v
                        for k, v in need.items():
                            if known.get(k, 0) >= v:
                                continue
                            e.wait_ge(sems[k], v)
                            known[k] = v
                        if o.fn is None:
                            continue
                        ins = o.fn(e)
                        if o.val is not None:
                            if isinstance(ins, (list, tuple)):
                                for i_ in ins:
                                    i_.then_inc(sems[o.val[0]], 16)
                            else:
                                ins.then_inc(sems[o.val[0]], 16 if o.is_dma else 1)
                return body

            block.tensor(run_stream("pe"))
            block.scalar(run_stream("act"))
            block.vector(run_stream("dve"))
            block.gpsimd(run_stream("pool"))
            block.sync(run_stream("sp"))


class Arena:
    def __init__(self, ap):
        self.ap = ap
        self.off = 0

    def reset(self, off=0):
        self.off = off

    def f32(self, n, parts=128):
        a = self.ap[0:parts, self.off:self.off + n]
        self.off += n
        assert self.off <= ARENA_WORDS, "SBUF arena overflow %d" % self.off
        return a

    def bf16(self, n, parts=128):
        assert n % 2 == 0
        return self.f32(n // 2, parts).bitcast(BF16)

    def i32(self, n, parts=128):
        return self.f32(n, parts).bitcast(I32)


def build(T, debug=False):
    NB = T // 256
    NI = NB // 4
    NPAIR = NI // 2
    NOWN = T // 4
    NTA = T // 512
    NTB = NOWN // 512
    assert NI % 2 == 0 and NB <= 64

    nc = bass.Bass("TRN2", target_bir_lowering=False)

    def din(name, shape, dt=F32):
        return nc.dram_tensor(name, shape, dt, kind="ExternalInput").ap()

    xT_all = din("xT_all", [2048, T])
    xT_own = din("xT_own", [2048, NOWN])
    x_own = din("x_own", [NOWN, 2048])
    pos_all = din("pos_all", [1, T], I32)
    pos_own = din("pos_own", [1, NOWN], I32)
    w_in = din("w_in", [2048, 6656])
    w_out = din("w_out", [2048, 2048])
    w_mkv = din("w_mkv", [2048, 1024])
    memT = din("memT", [2048, 256])
    glng = din("glng", [1, 512])
    glnb = din("glnb", [1, 512])
    wsT = din("wsT", [128, 512])
    bs = din("bs", [1, 512])
    lng = din("lng", [1, 2048])
    lnb = din("lnb", [1, 2048])
    invf = din("invf", [128, 1])
    pastneg = din("pastneg", [1, NTB * 4 * NB])
    notown = din("notown", [1, NTB * 4 * NB])
    m2d = din("m2d", [128, 8 * 2 * 512])
    out = nc.dram_tensor("out", [NOWN, 2048], F32, kind="ExternalOutput").ap()

    skind = "ExternalOutput" if debug else "Internal"
    Kscr = nc.dram_tensor("Kscr", [8, 128, T], BF16, kind=skind).ap()
    Vscr = nc.dram_tensor("Vscr", [T, 1024], BF16, kind=skind).ap()
    Qscr = nc.dram_tensor("Qscr", [8, 128, NOWN], BF16, kind=skind).ap()
    Gscr = nc.dram_tensor("Gscr", [8, 128, NOWN], BF16, kind=skind).ap()
    RTscr = nc.dram_tensor("RTscr", [8, NB, NOWN], BF16, kind=skind).ap()
    Yscr = nc.dram_tensor("Yscr", [2048, NOWN], BF16, kind=skind).ap()

    with contextlib.ExitStack() as st:
        arena_t = st.enter_context(nc.sbuf_tensor("arena", [128, ARENA_WORDS], F32))
        A = Arena(arena_t[:])
        ps_all = st.enter_context(nc.psum_tensor("ps_all", [128, 4096], F32))[:]
        pbank = [ps_all[:, i * 512:(i + 1) * 512] for i in range(8)]
        P = Prog(nc)
        uid = [0]

        def U(prefix):
            uid[0] += 1
            return (prefix, uid[0])

        ident = A.bf16(128)
        ones_bf = A.bf16(128)
        ones_f = A.f32(128)
        invf_s = A.f32(1)
        eps_s = A.f32(1)
        kmean = A.f32(8 * NB)
        kmean_bf = A.bf16(8 * NB)
        PERSIST = A.off

        P.op("pool", lambda e: e.memset(ident, 1.0), [], ["ident"])
        P.op("pool", lambda e: e.affine_select(out=ident, in_=ident, pattern=[[-1, 128]], compare_op=ALU.is_equal,
                                               fill=0.0, base=0, channel_multiplier=1), ["ident"], ["ident"])
        P.op("pool", lambda e: e.memset(ones_bf, 1.0), [], ["ones_bf"])
        P.op("pool", lambda e: e.memset(ones_f, 1.0), [], ["ones_f"])
        P.op("pool", lambda e: e.memset(eps_s, LN_EPS), [], ["eps"])
        P.dma("sp", invf_s, invf, writes=["invf"], semkey="invf")

        cast_rr = [0]

        def cast(dst, src, reads, writes, engs=("act", "pool")):
            e_ = engs[cast_rr[0] % len(engs)]
            cast_rr[0] += 1
            if e_ == "act":
                P.op("act", lambda e: e.copy(out=dst, in_=src), reads, writes)
            else:
                P.op(e_, lambda e: e.tensor_copy(out=dst, in_=src), reads, writes)

        class Stage:
            def __init__(self, n):
                self.slots = [A.f32(2048) for _ in range(n)]
                self.i = 0

            def next(self):
                s = self.i % len(self.slots)
                self.i += 1
                return s, self.slots[s]

        def load_weights(stage, dst, w_dram, col0, ncols, wkey):
            dv = dst.rearrange("p (c n) -> p c n", c=16)
            for dc in range(16):
                c0 = 0
                while c0 < ncols:
                    n = min(2048, ncols - c0)
                    s, buf = stage.next()
                    P.dma("sp", buf[:, 0:n], w_dram[dc * 128:(dc + 1) * 128, col0 + c0:col0 + c0 + n],
                          writes=[("stg", s)], semkey=("stg", s))
                    cast(dv[:, dc, c0:c0 + n], buf[:, 0:n], [("stg", s)], [(wkey, dc, c0)], engs=("act", "pool", "dve"))
                    c0 += n
            return [(wkey, dc, c0) for dc in range(16) for c0 in range(0, ncols, 2048)]

        def load_xT(stage, xb, src, t0, xkey, engs=("act",)):
            sv = src.rearrange("(c p) t -> p c t", p=128)
            xv = xb.rearrange("p (c t) -> p c t", c=16)
            for qd in range(4):
                s, buf = stage.next()
                bv = buf.rearrange("p (c t) -> p c t", c=4)
                P.dma("sp", bv, sv[:, qd * 4:(qd + 1) * 4, t0:t0 + 512], writes=[("stg", s)], semkey=("stg", s))
                cast(xv[:, qd * 4:(qd + 1) * 4, :], bv, [("stg", s)], [tuple(xkey) + (qd,)], engs=engs)
            return [tuple(xkey) + (qd,) for qd in range(4)]

        class RopeTab:
            def __init__(self):
                self.posb = A.i32(512)
                self.posf = A.f32(512)
                self.ang = [A.f32(512) for _ in range(2)]
                self.t2 = [A.f32(512) for _ in range(2)]
                self.r = [A.f32(512) for _ in range(2)]
                self.cs = [A.f32(512) for _ in range(2)]
                self.sn = [A.f32(512) for _ in range(2)]

            def make(self, pos_dram, t0, slot):
                T_ = self
                P.dma("sp", T_.posb, pos_dram[:, t0:t0 + 512].partition_broadcast(128), writes=["posb"], semkey="posb")
                P.op("dve", lambda e: e.tensor_copy(out=T_.posf, in_=T_.posb), ["posb"], ["posf"])
                P.op("dve", lambda e: e.tensor_scalar(out=T_.ang[0], in0=T_.posf, scalar1=invf_s[:, 0:1], scalar2=None,
                                                      op0=ALU.mult), ["posf", "invf"], [("ang", 0)])
                P.op("dve", lambda e: e.tensor_scalar(out=T_.ang[1], in0=T_.ang[0], scalar1=float(np.pi / 2), scalar2=None,
                                                      op0=ALU.add), [("ang", 0)], [("ang", 1)])
                for w in (0, 1):
                    ang, t2, r = T_.ang[w], T_.t2[w], T_.r[w]
                    P.op("dve", lambda e, ang=ang, t2=t2: e.tensor_scalar(out=t2, in0=ang, scalar1=float(1.0 / TWO_PI), scalar2=MAGIC,
                                                                          op0=ALU.mult, op1=ALU.add), [("ang", w)], [("t2", w)])
                    P.op("dve", lambda e, t2=t2: e.tensor_scalar(out=t2, in0=t2, scalar1=-MAGIC, scalar2=None, op0=ALU.add),
                         [("t2", w)], [("t2", w)])
                    P.op("dve", lambda e, ang=ang, t2=t2, r=r: e.scalar_tensor_tensor(out=r, in0=t2, scalar=-C1, in1=ang, op0=ALU.mult,
                                                                                      op1=ALU.add), [("t2", w), ("ang", w)], [("r", w)])
                    P.op("dve", lambda e, t2=t2, r=r: e.scalar_tensor_tensor(out=r, in0=t2, scalar=-C2, in1=r, op0=ALU.mult,
                                                                             op1=ALU.add), [("t2", w), ("r", w)], [("r", w)])
                    P.op("dve", lambda e, r=r: e.tensor_scalar(out=r, in0=r, scalar1=float(np.pi), scalar2=float(-np.pi),
                                                               op0=ALU.min, op1=ALU.max), [("r", w)], [("r", w)])
                snl, csl = T_.sn[slot], T_.cs[slot]
                r0, r1 = T_.r[0], T_.r[1]
                P.op("act", lambda e: e.activation(out=snl[0:64, :], in_=r0[0:64, :], func=AF.Sin, scale=-1.0), [("r", 0)], [("sn", slot, 0)])
                P.op("act", lambda e: e.activation(out=snl[64:128, :], in_=r0[64:128, :], func=AF.Sin), [("r", 0)], [("sn", slot, 1)])
                P.op("act", lambda e: e.activation(out=csl, in_=r1, func=AF.Sin), [("r", 1)], [("cs", slot)])

        def rope(ps, pskey, tab, slot, u, v, outap, okey):
            cs, sn = tab.cs[slot], tab.sn[slot]
            P.op("dve", lambda e: e.tensor_tensor(out=u[0:64, :], in0=ps[64:128, :], in1=sn[0:64, :], op=ALU.mult),
                 [pskey, ("sn", slot, 0)], [(okey, "ul")])
            P.op("dve", lambda e: e.tensor_tensor(out=u[64:128, :], in0=ps[0:64, :], in1=sn[64:128, :], op=ALU.mult),
                 [pskey, ("sn", slot, 1)], [(okey, "uh")])
            P.op("dve", lambda e: e.tensor_tensor(out=v, in0=ps, in1=cs, op=ALU.mult), [pskey, ("cs", slot)], [(okey, "v")])
            P.op("dve", lambda e: e.tensor_tensor(out=outap, in0=v, in1=u, op=ALU.add),
                 [(okey, "ul"), (okey, "uh"), (okey, "v")], [okey])

        def mm_group(psap, pskey, pairs, extra_reads=()):
            n = len(pairs)
            for i, (l, r, rd) in enumerate(pairs):
                P.op("pe", lambda e, l=l, r=r, i=i: e.matmul(psap, lhsT=l, rhs=r, start=(i == 0), stop=(i == n - 1)),
                     list(rd) + list(extra_reads), [pskey])

        A.reset(PERSIST)
        stage = Stage(3)
        Wkv = A.bf16(16 * 2048)
        Wkv_v = Wkv.rearrange("p (c n) -> p c n", c=16)
        xbs = [A.bf16(16 * 512) for _ in range(2)]
        tab = RopeTab()
        ua = [A.f32(512) for _ in range(2)]
        va = [A.f32(512) for _ in range(2)]
        kr = [A.f32(512) for _ in range(2)]
        kb_all = A.bf16(8 * 512)
        kb_v = kb_all.rearrange("p (h t) -> p h t", h=8)
        vb_all = A.bf16(4 * 1024)
        vb_v = vb_all.rearrange("p (s e) -> p s e", s=4)
        kmean_v = kmean.rearrange("p (h n) -> p h n", h=8)

        wkeys = load_weights(stage, Wkv, w_in, 1024, 2048, "Wkv")
        Kscr_v = Kscr.rearrange("h p t -> p h t")

        load_xT(stage, xbs[0], xT_all, 0, ("xa", 0))
        tab.make(pos_all, 0, 0)
        psrr = [0]
        for tt in range(NTA):
            if tt + 1 < NTA:
                load_xT(stage, xbs[(tt + 1) % 2], xT_all, (tt + 1) * 512, ("xa", (tt + 1) % 2))
            xb = xbs[tt % 2].rearrange("p (c t) -> p c t", c=16)
            sl = tt % 2
            for h in range(8):
                b_ = psrr[0] % 4
                psrr[0] += 1
                ps = pbank[b_][:]
                mm_group(ps, ("ps", b_), [(Wkv_v[:, dc, h * 128:(h + 1) * 128], xb[:, dc, :],
                                          [("Wkv", dc, 0), ("xa", sl, dc // 4)]) for dc in range(16)])
                i2 = (tt * 8 + h) % 2
                rope(ps, ("ps", b_), tab, sl, ua[i2], va[i2], kr[i2], ("kr", i2))
                P.op("pool", lambda e, i2=i2, h=h: e.tensor_copy(out=kb_v[:, h, :], in_=kr[i2]), [("kr", i2)], [("kb", h)])
                P.op("dve", lambda e, i2=i2, h=h, tt=tt: e.tensor_reduce(
                    out=kmean_v[:, h, 2 * tt:2 * tt + 2], in_=kr[i2].rearrange("p (b k) -> p b k", b=2), axis=AX.X, op=ALU.add),
                    [("kr", i2)], [("kmean", h, tt)])
            P.dma("pool", Kscr_v[:, :, tt * 512:(tt + 1) * 512], kb_v, reads=[("kb", h) for h in range(8)],
                  writes=[("Kscr", tt)], semkey="kb")
            if tt + 1 < NTA:
                tab.make(pos_all, (tt + 1) * 512, (tt + 1) % 2)
            for sub in range(4):
                for half in range(2):
                    b_ = 4 + psrr[0] % 4
                    psrr[0] += 1
                    ps = pbank[b_][:]
                    mm_group(ps, ("ps", b_), [(xb[:, dc, sub * 128:(sub + 1) * 128], Wkv_v[:, dc, 1024 + half * 512:1024 + (half + 1) * 512],
                                              [("Wkv", dc, 0), ("xa", sl, dc // 4)]) for dc in range(16)])
                    P.op("act", lambda e, ps=ps, sub=sub, half=half: e.copy(out=vb_v[:, sub, half * 512:(half + 1) * 512], in_=ps),
                         [("ps", b_)], [("vb", sub, half)])
            P.dma("pool", Vscr[tt * 512:(tt + 1) * 512, :].rearrange("(s p) e -> p s e", p=128), vb_v,
                  reads=[("vb", s_, h_) for s_ in range(4) for h_ in range(2)], writes=[("Vscr", tt)], semkey="vb")
        P.op("dve", lambda e: e.tensor_scalar(out=kmean_bf, in0=kmean, scalar1=1.0 / 256.0, scalar2=None, op0=ALU.mult),
             [("kmean", h, tt) for h in range(8) for tt in range(NTA)], ["kmean_bf"])
        P.barrier()

        kmean_bf_v = kmean_bf.rearrange("p (h n) -> p h n", h=8)
        Qscr_v = Qscr.rearrange("h p t -> p h t")
        Gscr_v = Gscr.rearrange("h p t -> p h t")

        A.reset(PERSIST)
        stage = Stage(3)
        Wq = A.bf16(16 * 1024)
        Wq_v = Wq.rearrange("p (c n) -> p c n", c=16)
        xbs = [A.bf16(16 * 512) for _ in range(2)]
        tab = RopeTab()
        ua = [A.f32(512) for _ in range(2)]
        va = [A.f32(512) for _ in range(2)]
        qr = [A.f32(512) for _ in range(2)]
        qb_all = A.bf16(8 * 512)
        qb_v = qb_all.rearrange("p (h t) -> p h t", h=8)
        pastneg_s = A.f32(NTB * 4 * NB)
        notown_s = A.f32(NTB * 4 * NB)
        pastneg_v = pastneg_s.rearrange("p (t s n) -> p t s n", t=NTB, s=4)
        notown_v = notown_s.rearrange("p (t s n) -> p t s n", t=NTB, s=4)
        gm4 = [A.f32(4 * NB) for _ in range(2)]
        top84 = [A.f32(32) for _ in range(2)]
        thr4 = [A.f32(4) for _ in range(2)]
        lt4 = [A.f32(4 * NB) for _ in range(2)]
        rmask4 = [A.bf16(4 * NB) for _ in range(2)]
        rt_all = A.bf16(8 * 512)
        rt_v = rt_all.rearrange("p (h t) -> p h t", h=8)
        RTscr_v = RTscr.rearrange("h s t -> s h t")
        P.dma("sp", pastneg_s, pastneg.partition_broadcast(128), writes=["pastneg"], semkey="pastneg")
        P.dma("sp", notown_s, notown.partition_broadcast(128), writes=["notown"], semkey="notown")
        load_weights(stage, Wq, w_in, 0, 1024, "Wq")
        load_xT(stage, xbs[0], xT_own, 0, ("xo", 0))
        tab.make(pos_own, 0, 0)
        ptb = pbank[7].bitcast(BF16)
        for tt in range(NTB):
            if tt + 1 < NTB:
                load_xT(stage, xbs[(tt + 1) % 2], xT_own, (tt + 1) * 512, ("xo", (tt + 1) % 2))
            xb = xbs[tt % 2].rearrange("p (c t) -> p c t", c=16)
            sl = tt % 2
            for step in range(10):
                if step < 8:
                    h = step
                    b_ = psrr[0] % 4
                    psrr[0] += 1
                    ps = pbank[b_][:]
                    mm_group(ps, ("ps", b_), [(Wq_v[:, dc, h * 128:(h + 1) * 128], xb[:, dc, :],
                                              [("Wq", dc, 0), ("xo", sl, dc // 4)]) for dc in range(16)])
                    i2 = h % 2
                    rope(ps, ("ps", b_), tab, sl, ua[i2], va[i2], qr[i2], ("qr", i2))
                    P.op("act", lambda e, i2=i2, h=h: e.copy(out=qb_v[:, h, :], in_=qr[i2]), [("qr", i2)], [("qb", h)])
                if 1 <= step <= 8:
                    h = step - 1
                    g2 = h % 2
                    pg = pbank[4 + g2][:, 0:4 * NB]
                    for sub in range(4):
                        P.op("pe", lambda e, pg=pg, h=h, sub=sub: e.matmul(pg[:, sub * NB:(sub + 1) * NB], lhsT=qb_v[:, h, sub * 128:(sub + 1) * 128],
                                                                           rhs=kmean_bf_v[:, h, :], start=True, stop=True),
                             [("qb", h), "kmean_bf"], [("ps", 4 + g2)])
                    gmv = gm4[g2].rearrange("p (s n) -> p s n", s=4)
                    P.op("dve", lambda e, pg=pg, gmv=gmv, tt=tt: e.tensor_tensor(out=gmv, in0=pg.rearrange("p (s n) -> p s n", s=4),
                                                                                in1=pastneg_v[:, tt], op=ALU.add),
                         [("ps", 4 + g2), "pastneg"], [("gm", g2)])
                    for sub in range(4):
                        P.op("dve", lambda e, g2=g2, sub=sub: e.max(out=top84[g2][:, sub * 8:(sub + 1) * 8], in_=gm4[g2][:, sub * NB:(sub + 1) * NB]),
                             [("gm", g2)], [("top8", g2, sub)])
                    P.op("dve", lambda e, g2=g2: e.tensor_scalar(out=thr4[g2], in0=top84[g2].rearrange("p (s e) -> p s e", s=4)[:, :, 2],
                                                                 scalar1=-1e29, scalar2=None, op0=ALU.max),
                         [("top8", g2, s_) for s_ in range(4)], [("thr", g2)])
                    ltv = lt4[g2].rearrange("p (s n) -> p s n", s=4)
                    P.op("dve", lambda e, g2=g2, gmv=gmv, ltv=ltv: e.tensor_tensor(out=ltv, in0=gmv, in1=thr4[g2].unsqueeze(2).to_broadcast([128, 4, NB]),
                                                                                  op=ALU.is_lt), [("gm", g2), ("thr", g2)], [("lt", g2)])
                    P.op("dve", lambda e, g2=g2, ltv=ltv, tt=tt: e.tensor_tensor(out=rmask4[g2].rearrange("p (s n) -> p s n", s=4), in0=ltv,
                                                                                in1=notown_v[:, tt], op=ALU.mult),
                         [("lt", g2), "notown"], [("rmask", g2)])
                if 2 <= step <= 9:
                    h = step - 2
                    g2 = h % 2
                    for sub in range(4):
                        pt = ptb[0:NB, g2 * 512 + sub * 128:g2 * 512 + (sub + 1) * 128]
                        P.op("pe", lambda e, pt=pt, g2=g2, sub=sub: e.transpose(pt, rmask4[g2][:, sub * NB:(sub + 1) * NB], ident),
                             [("rmask", g2), "ident"], [("ptb", g2)])
                    P.op("act", lambda e, g2=g2, h=h: e.copy(out=rt_v[0:NB, h, :], in_=ptb[0:NB, g2 * 512:(g2 + 1) * 512]),
                         [("ptb", g2)], [("rt", h)])
            if tt + 1 < NTB:
                tab.make(pos_own, (tt + 1) * 512, (tt + 1) % 2)
            P.dma("pool", Qscr_v[:, :, tt * 512:(tt + 1) * 512], qb_v, reads=[("qb", h) for h in range(8)],
                  writes=[("Qscr", tt)], semkey="qb")
            P.dma("pool", RTscr_v[:, :, tt * 512:(tt + 1) * 512], rt_v[0:NB], reads=[("rt", h) for h in range(8)],
                  writes=[("RTscr", tt)], semkey="rt")
        P.barrier()

        A.reset(PERSIST)
        stage = Stage(3)
        Wg = A.bf16(16 * 1024)
        Wg_v = Wg.rearrange("p (c n) -> p c n", c=16)
        xbs = [A.bf16(16 * 512) for _ in range(2)]
        gb_all = A.bf16(8 * 512)
        gb_v = gb_all.rearrange("p (h t) -> p h t", h=8)
        load_weights(stage, Wg, w_in, 3072, 1024, "Wg")
        load_xT(stage, xbs[0], xT_own, 0, ("xo", 0))
        for tt in range(NTB):
            if tt + 1 < NTB:
                load_xT(stage, xbs[(tt + 1) % 2], xT_own, (tt + 1) * 512, ("xo", (tt + 1) % 2))
            xb = xbs[tt % 2].rearrange("p (c t) -> p c t", c=16)
            sl = tt % 2
            for h in range(8):
                b_ = psrr[0] % 4
                psrr[0] += 1
                ps = pbank[b_][:]
                mm_group(ps, ("ps", b_), [(Wg_v[:, dc, h * 128:(h + 1) * 128], xb[:, dc, :],
                                          [("Wg", dc, 0), ("xo", sl, dc // 4)]) for dc in range(16)])
                P.op("act", lambda e, ps=ps, h=h: e.activation(out=gb_v[:, h, :], in_=ps, func=AF.Silu), [("ps", b_)], [("gb", h)])
            P.dma("pool", Gscr_v[:, :, tt * 512:(tt + 1) * 512], gb_v, reads=[("gb", h) for h in range(8)],
                  writes=[("Gscr", tt)], semkey="gb")
        P.barrier()

        A.reset(PERSIST)
        stage = Stage(3)
        W3 = A.bf16(16 * 1536)
        W3_v = W3.rearrange("p (c n) -> p c n", c=16)
        xbs = [A.bf16(16 * 512) for _ in range(2)]
        glng_s = A.f32(512)
        glnb_s = A.f32(512)
        bs_s = A.f32(512)
        ws_f = A.f32(512)
        ws_b = A.bf16(512)
        ws_bv = ws_b.rearrange("p (g t) -> p g t", g=4)
        ug = A.f32(4 * 512)
        ug_v = ug.rearrange("p (g t) -> p g t", g=4)
        sg = A.f32(4 * 512)
        sg_v = sg.rearrange("p (g t) -> p g t", g=4)
        vg = [A.f32(512) for _ in range(2)]
        junk = A.f32(512)
        s1 = [A.f32(1) for _ in range(2)]
        s2 = [A.f32(1) for _ in range(2)]
        mu = [A.f32(1) for _ in range(2)]
        var = [A.f32(1) for _ in range(2)]
        rstd = [A.f32(1) for _ in range(2)]
        vln = [A.bf16(512) for _ in range(2)]
        mx = [A.f32(512) for _ in range(2)]
        ygm = A.bf16(4 * 512)
        ygm_v = ygm.rearrange("p (g t) -> p g t", g=4)
        P.dma("sp", glng_s, glng.partition_broadcast(128), writes=["glng"], semkey="glng")
        P.dma("sp", glnb_s, glnb.partition_broadcast(128), writes=["glnb"], semkey="glnb")
        P.dma("sp", bs_s, bs.partition_broadcast(128), writes=["bs"], semkey="bs")
        P.dma("sp", ws_f, wsT, writes=["ws_f"], semkey="ws_f")
        P.op("pool", lambda e: e.affine_select(out=ws_f.rearrange("p (g t) -> p g t", g=4), in_=ws_f.rearrange("p (g t) -> p g t", g=4),
                                               pattern=[[0, 4], [1, 128]], compare_op=ALU.is_ge, fill=0.0, base=0,
                                               channel_multiplier=-1), ["ws_f"], ["ws_f"])
        P.op("pool", lambda e: e.tensor_copy(out=ws_b, in_=ws_f), ["ws_f"], ["ws_b"])
        load_weights(stage, W3, w_in, 4096, 1536, "W3")
        load_xT(stage, xbs[0], xT_own, 0, ("xo", 0))
        Yscr_g = Yscr[1024:1536, :].rearrange("(g p) t -> p g t", p=128)
        for tt in range(NTB):
            if tt + 1 < NTB:
                load_xT(stage, xbs[(tt + 1) % 2], xT_own, (tt + 1) * 512, ("xo", (tt + 1) % 2))
            xb = xbs[tt % 2].rearrange("p (c t) -> p c t", c=16)
            sl = tt % 2
            for g in range(4):
                for which, c0, func, dstv, dk in ((0, 0, AF.Gelu, ug_v, "ug"), (1, 1024, AF.Silu, sg_v, "sg")):
                    b_ = psrr[0] % 4
                    psrr[0] += 1
                    ps = pbank[b_][:]
                    mm_group(ps, ("ps", b_), [(W3_v[:, dc, c0 + g * 128:c0 + (g + 1) * 128], xb[:, dc, :],
                                              [("W3", dc, 0), ("xo", sl, dc // 4)]) for dc in range(16)])
                    P.op("act", lambda e, ps=ps, g=g, func=func, dstv=dstv: e.activation(out=dstv[:, g, :], in_=ps, func=func),
                         [("ps", b_)], [(dk, g)])
            mix_pend = None
            for sub in range(4):
                k2 = sub % 2
                b_ = psrr[0] % 4
                psrr[0] += 1
                ps = pbank[b_][:]
                mm_group(ps, ("ps", b_), [(xb[:, dc, sub * 128:(sub + 1) * 128], W3_v[:, dc, 512:1024],
                                          [("W3", dc, 0), ("xo", sl, dc // 4)]) for dc in range(16)])
                P.op("act", lambda e, ps=ps, k2=k2: e.activation(out=vg[k2], in_=ps, func=AF.Gelu), [("ps", b_)], [("vg", k2)])
                if mix_pend is not None:
                    mix_pend()
                P.op("dve", lambda e, k2=k2: e.tensor_reduce(out=s1[k2], in_=vg[k2], axis=AX.X, op=ALU.add), [("vg", k2)], [("s1", k2)])
                P.op("dve", lambda e, k2=k2: e.scalar_tensor_tensor(out=junk, in0=vg[k2], scalar=1.0, in1=vg[k2], op0=ALU.mult,
                                                                    op1=ALU.mult, accum_out=s2[k2]), [("vg", k2)], ["junk", ("s2", k2)])
                P.op("dve", lambda e, k2=k2: e.tensor_scalar(out=mu[k2], in0=s1[k2], scalar1=1.0 / 512.0, scalar2=None, op0=ALU.mult),
                     [("s1", k2)], [("mu", k2)])
                P.op("dve", lambda e, k2=k2: e.tensor_tensor(out=var[k2], in0=mu[k2], in1=mu[k2], op=ALU.mult), [("mu", k2)], [("var", k2)])
                P.op("dve", lambda e, k2=k2: e.scalar_tensor_tensor(out=var[k2], in0=s2[k2], scalar=1.0 / 512.0, in1=var[k2],
                                                                    op0=ALU.mult, op1=ALU.subtract), [("s2", k2), ("var", k2)], [("var", k2)])
                P.op("act", lambda e, k2=k2: e.activation(out=rstd[k2], in_=var[k2], func=AF.Sqrt, bias=eps_s[:, 0:1], scale=1.0),
                     [("var", k2), "eps"], [("rstd", k2)])
                P.op("dve", lambda e, k2=k2: e.reciprocal(out=rstd[k2], in_=rstd[k2]), [("rstd", k2)], [("rstd", k2)])
                P.op("dve", lambda e, k2=k2: e.tensor_scalar(out=vg[k2], in0=vg[k2], scalar1=mu[k2][:, 0:1], scalar2=rstd[k2][:, 0:1],
                                                             op0=ALU.subtract, op1=ALU.mult), [("vg", k2), ("mu", k2), ("rstd", k2)], [("vg", k2)])
                P.op("pool", lambda e, k2=k2: e.tensor_tensor(out=vg[k2], in0=vg[k2], in1=glng_s, op=ALU.mult), [("vg", k2), "glng"], [("vg", k2)])
                P.op("pool", lambda e, k2=k2: e.tensor_tensor(out=vln[k2], in0=vg[k2], in1=glnb_s, op=ALU.add), [("vg", k2), "glnb"], [("vln", k2)])
                def mix(sub=sub, k2=k2):
                    bm = 4 + (psrr[0] % 2)
                    psm = pbank[bm][:]
                    for g in range(4):
                        P.op("pe", lambda e, psm=psm, g=g, k2=k2: e.matmul(psm[:, g * 128:(g + 1) * 128], lhsT=vln[k2][:, g * 128:(g + 1) * 128],
                                                                           rhs=ws_bv[:, g, :], start=True, stop=True),
                             [("vln", k2), "ws_b"], [("ps", bm)])
                    P.op("dve", lambda e, psm=psm, k2=k2: e.tensor_tensor(out=mx[k2], in0=psm, in1=bs_s, op=ALU.add), [("ps", bm), "bs"], [("mx", k2)])
                    mxv = mx[k2].rearrange("p (g t) -> p g t", g=4)
                    P.op("dve", lambda e, mxv=mxv, sub=sub: e.tensor_tensor(out=mxv, in0=mxv, in1=ug_v[:, :, sub * 128:(sub + 1) * 128], op=ALU.mult),
                         [("mx", k2)] + [("ug", g) for g in range(4)], [("mx", k2)])
                    P.op("pool", lambda e, mxv=mxv, sub=sub: e.tensor_tensor(out=ygm_v[:, :, sub * 128:(sub + 1) * 128], in0=mxv,
                                                                             in1=sg_v[:, :, sub * 128:(sub + 1) * 128], op=ALU.mult),
                         [("mx", k2)] + [("sg", g) for g in range(4)], [("ygm", sub)])
                mix_pend = mix
            mix_pend()
            P.dma("pool", Yscr_g[:, :, tt * 512:(tt + 1) * 512], ygm_v, reads=[("ygm", s_) for s_ in range(4)],
                  writes=[("Yscr_g", tt)], semkey="ygm")
        P.barrier()

        A.reset(PERSIST)
        stage = Stage(3)
        W4 = A.bf16(16 * 1024)
        W4_v = W4.rearrange("p (c n) -> p c n", c=16)
        Wm = A.bf16(16 * 1024)
        Wm_v = Wm.rearrange("p (c n) -> p c n", c=16)
        mT_f = A.f32(16 * 256)
        mT_b = A.bf16(16 * 256)
        mT_v = mT_b.rearrange("p (c m) -> p c m", c=16)
        memKT = A.bf16(4 * 256)
        memKT_v = memKT.rearrange("p (h m) -> p h m", h=4)
        memV = A.bf16(2 * 512)
        memV_v = memV.rearrange("p (c e) -> p c e", c=2)
        xbs = [A.bf16(16 * 512) for _ in range(2)]
        qm = [A.bf16(512) for _ in range(2)]
        sgm = [A.f32(512) for _ in range(2)]
        pT = [A.bf16(512) for _ in range(4)]
        rec = [A.f32(512) for _ in range(2)]
        yt = [A.f32(512) for _ in range(2)]
        yme = A.bf16(4 * 512)
        yme_v = yme.rearrange("p (h t) -> p h t", h=4)
        P.dma("sp", mT_f.rearrange("p (c m) -> p c m", c=16), memT.rearrange("(c p) m -> p c m", p=128), writes=["mT_f"], semkey="mT_f")
        P.op("dve", lambda e: e.tensor_copy(out=mT_b, in_=mT_f), ["mT_f"], ["mT_b"])
        wmk = load_weights(stage, Wm, w_mkv, 0, 1024, "Wm")
        load_weights(stage, W4, w_in, 5632, 1024, "W4")
        for hh in range(4):
            b_ = psrr[0] % 4
            psrr[0] += 1
            ps = pbank[b_][:, 0:256]
            mm_group(ps, ("ps", b_), [(Wm_v[:, dc, hh * 128:(hh + 1) * 128], mT_v[:, dc, :], [("Wm", dc, 0), "mT_b"]) for dc in range(16)])
            P.op("act", lambda e, ps=ps, hh=hh: e.copy(out=memKT_v[:, hh, :], in_=ps), [("ps", b_)], [("memKT", hh)])
        for mc in range(2):
            b_ = psrr[0] % 4
            psrr[0] += 1
            ps = pbank[b_][:]
            mm_group(ps, ("ps", b_), [(mT_v[:, dc, mc * 128:(mc + 1) * 128], Wm_v[:, dc, 512:1024], [("Wm", dc, 0), "mT_b"]) for dc in range(16)])
            P.op("act", lambda e, ps=ps, mc=mc: e.copy(out=memV_v[:, mc, :], in_=ps), [("ps", b_)], [("memV", mc)])
        load_xT(stage, xbs[0], xT_own, 0, ("xo", 0))
        Yscr_m = Yscr[1536:2048, :].rearrange("(h p) t -> p h t", p=128)
        for tt in range(NTB):
            if tt + 1 < NTB:
                load_xT(stage, xbs[(tt + 1) % 2], xT_own, (tt + 1) * 512, ("xo", (tt + 1) % 2))
            xb = xbs[tt % 2].rearrange("p (c t) -> p c t", c=16)
            sl = tt % 2
            for hh in range(4):
                k2 = hh % 2
                b_ = psrr[0] % 2
                psrr[0] += 1
                ps = pbank[b_][:]
                mm_group(ps, ("ps", b_), [(W4_v[:, dc, hh * 128:(hh + 1) * 128], xb[:, dc, :],
                                          [("W4", dc, 0), ("xo", sl, dc // 4)]) for dc in range(16)])
                P.op("dve", lambda e, ps=ps, k2=k2: e.tensor_copy(out=qm[k2], in_=ps), [("ps", b_)], [("qm", k2)])
                bg = 2 + k2
                psg = pbank[bg][:]
                mm_group(psg, ("ps", bg), [(W4_v[:, dc, 512 + hh * 128:512 + (hh + 1) * 128], xb[:, dc, :],
                                           [("W4", dc, 0), ("xo", sl, dc // 4)]) for dc in range(16)])
                P.op("act", lambda e, psg=psg, k2=k2: e.activation(out=sgm[k2], in_=psg, func=AF.Silu), [("ps", bg)], [("sgm", k2)])
                for mc in range(2):
                    bsn = 4 + mc
                    pss = pbank[bsn][:]
                    P.op("pe", lambda e, pss=pss, hh=hh, mc=mc, k2=k2: e.matmul(pss, lhsT=memKT_v[:, hh, mc * 128:(mc + 1) * 128], rhs=qm[k2],
                                                                               start=True, stop=True), [("memKT", hh), ("qm", k2)], [("ps", bsn)])
                    pi = k2 * 2 + mc
                    P.op("act", lambda e, pss=pss, pi=pi: e.activation(out=pT[pi], in_=pss, func=AF.Exp, scale=SCALE), [("ps", bsn)], [("pT", pi)])
                pso = pbank[6][:]
                psd = pbank[7][:]
                mm_group(pso, ("ps", 6), [(memV_v[:, mc, hh * 128:(hh + 1) * 128], pT[k2 * 2 + mc], [("memV", mc), ("pT", k2 * 2 + mc)]) for mc in range(2)])
                mm_group(psd, ("ps", 7), [(ones_bf, pT[k2 * 2 + mc], ["ones_bf", ("pT", k2 * 2 + mc)]) for mc in range(2)])
                P.op("dve", lambda e, psd=psd, k2=k2: e.reciprocal(out=rec[k2], in_=psd), [("ps", 7)], [("rec", k2)])
                P.op("dve", lambda e, pso=pso, k2=k2: e.tensor_tensor(out=yt[k2], in0=pso, in1=rec[k2], op=ALU.mult), [("ps", 6), ("rec", k2)], [("yt", k2)])
                P.op("pool", lambda e, k2=k2, hh=hh: e.tensor_tensor(out=yme_v[:, hh, :], in0=yt[k2], in1=sgm[k2], op=ALU.mult),
                     [("yt", k2), ("sgm", k2)], [("yme", hh)])
            P.dma("pool", Yscr_m[:, :, tt * 512:(tt + 1) * 512], yme_v, reads=[("yme", h_) for h_ in range(4)],
                  writes=[("Yscr_m", tt)], semkey="yme")
        P.barrier()

        A.reset(PERSIST)
        Kh = A.bf16(NB * 256)
        Kh_v = Kh.rearrange("p (s k) -> p s k", s=NB)
        Vh = A.bf16(NB * 256)
        Vh_v = Vh.rearrange("p (s c d) -> p s c d", s=NB, c=2)
        Qh = [A.bf16(NOWN) for _ in range(2)]
        Gh = [A.bf16(NOWN) for _ in range(2)]
        RTh = [A.bf16(NOWN) for _ in range(2)]
        Esel = A.bf16(NB * 128)
        Esel_v = Esel.rearrange("p (s k) -> p s k", s=NB)
        m2_f = [A.f32(2048) for _ in range(1)]
        m2_b = A.bf16(8 * 2 * 512)
        m2_v = m2_b.rearrange("p (r c q) -> p r c q", r=8, c=2)
        PT2 = [A.bf16(1024) for _ in range(3)]
        accD = A.f32(1024)
        recm = A.f32(512)
        ytm = A.f32(512)
        ymo = [A.bf16(512) for _ in range(2)]
        P.op("pool", lambda e: e.memset(Esel, 1.0), [], ["Esel"])
        P.op("pool", lambda e: e.affine_select(out=Esel_v, in_=Esel_v, pattern=[[-1, NB], [0, 128]], compare_op=ALU.is_equal,
                                               fill=0.0, base=0, channel_multiplier=1), ["Esel"], ["Esel"])
        for hs_ in range(2):
            P.op("pool", lambda e, hs_=hs_: e.memset(RTh[hs_], 0.0), [], [("RTh", hs_)])
        for i4 in range(4):
            P.dma("sp", m2_f[0], m2d[:, i4 * 2048:(i4 + 1) * 2048], writes=["m2_f"], semkey="m2_f")
            P.op("dve", lambda e, i4=i4: e.tensor_copy(out=m2_b[:, i4 * 2048:(i4 + 1) * 2048], in_=m2_f[0]), ["m2_f"], [("m2_b", i4)])
        m2keys = [("m2_b", i4) for i4 in range(4)]
        Vscr_v = Vscr.rearrange("(s c k) (h d) -> h k s c d", c=2, k=128, d=128)
        Kscr_s = Kscr.rearrange("h p (s k) -> h p s k", k=256)
        Yscr_h = Yscr[0:1024, :].rearrange("(h p) t -> h p t", p=128)
        nstep = 0
        for h in range(8):
            hs = h % 2
            P.dma("sp", Qh[hs], Qscr[h], reads=[("Qscr", tt) for tt in range(NTB)], writes=[("Qh", hs)], semkey=("Qh", hs))
            P.dma("sp", Gh[hs], Gscr[h], reads=[("Gscr", tt) for tt in range(NTB)], writes=[("Gh", hs)], semkey=("Gh", hs))
            P.dma("sp", RTh[hs][0:NB], RTscr[h], reads=[("RTscr", tt) for tt in range(NTB)], writes=[("RTh", hs)], semkey=("RTh", hs))
            for g in range(NB // 8 - 1, -1, -1):
                P.op("sp", lambda e, g=g, h=h: [e.dma_start(out=Kh_v[:, g * 8:(g + 1) * 8, :], in_=Kscr_s[h, :, g * 8:(g + 1) * 8, :]),
                                                e.dma_start(out=Vh_v[:, g * 8:(g + 1) * 8], in_=Vscr_v[h, :, g * 8:(g + 1) * 8])],
                     reads=[("Kscr", t_) for t_ in range(g * 4, g * 4 + 4)] + [("Vscr", t_) for t_ in range(g * 4, g * 4 + 4)],
                     writes=[("Kh", g), ("Vh", g)], semkey=("KV", g), ndma=2)
            for p in range(NPAIR - 1, -1, -1):
                q0 = p * 512
                nsl = 8 * p + 8
                ob = 6
                psO = pbank[ob][:]
                nstep += 1
                used = {"accD": False, "accP": False}

                def qk(s, j):
                    p3 = j % 3
                    for c in range(2):
                        bS = p3 * 2 + c
                        psS = pbank[bS][:]
                        terms = [(Kh_v[:, s, c * 128:(c + 1) * 128], Qh[hs][:, q0:q0 + 512], [("Kh", s // 8), ("Qh", hs)]),
                                 (Esel_v[:, s, :], RTh[hs][:, q0:q0 + 512], ["Esel", ("RTh", hs)])]
                        if s >= 8 * p:
                            terms.append((ident, m2_v[:, s - 8 * p, c, :], ["ident"] + m2keys))
                        mm_group(psS, ("ps", bS), terms)
                    bS0 = p3 * 2
                    P.op("act", lambda e, bS0=bS0, p3=p3: e.activation(out=PT2[p3], in_=ps_all[:, bS0 * 512:(bS0 + 2) * 512], func=AF.Exp, scale=SCALE),
                         [("ps", bS0), ("ps", bS0 + 1)], [("PT", p3)])
                    if not used["accD"]:
                        used["accD"] = True
                        P.op("dve", lambda e, p3=p3: e.tensor_copy(out=accD, in_=PT2[p3]), [("PT", p3)], ["accD"])
                    else:
                        P.op("dve", lambda e, p3=p3: e.tensor_tensor(out=accD, in0=accD, in1=PT2[p3], op=ALU.add), [("PT", p3), "accD"], ["accD"])

                def pv(s, j):
                    p3 = j % 3
                    for c in range(2):
                        P.op("pe", lambda e, c=c, p3=p3, s=s, j=j: e.matmul(psO, lhsT=Vh_v[:, s, c, :], rhs=PT2[p3][:, c * 512:(c + 1) * 512],
                                                                       start=(j == 0 and c == 0), stop=(j == nsl - 1 and c == 1)),
                             [("Vh", s // 8), ("PT", p3)], [("ps", ob)])

                order = list(range(nsl - 1, -1, -1))
                for j in range(nsl + 2):
                    if j < nsl:
                        qk(order[j], j)
                    if j >= 2:
                        pv(order[j - 2], j - 2)
                psD = pbank[7][:]
                mm_group(psD, ("ps", 7), [(ones_f, accD[:, 0:512], ["ones_f", "accD"]), (ones_f, accD[:, 512:1024], ["ones_f", "accD"])])
                P.op("dve", lambda e, psD=psD: e.reciprocal(out=recm, in_=psD), [("ps", 7)], ["recm"])
                P.op("dve", lambda e, psO=psO: e.tensor_tensor(out=ytm, in0=psO, in1=recm, op=ALU.mult), [("ps", ob), "recm"], ["ytm"])
                y2 = nstep % 2
                P.op("pool", lambda e, y2=y2, hs=hs, q0=q0: e.tensor_tensor(out=ymo[y2], in0=ytm, in1=Gh[hs][:, q0:q0 + 512], op=ALU.mult),
                     ["ytm", ("Gh", hs)], [("ymo", y2)])
                P.dma("pool", Yscr_h[h, :, q0:q0 + 512], ymo[y2], reads=[("ymo", y2)], writes=[("Yscr_h", h, p)], semkey=("ymo", y2))
        P.barrier()

        A.reset(PERSIST)
        stage = Stage(3)
        Wo = A.bf16(16 * 2048)
        Wo_v = Wo.rearrange("p (c n) -> p c n", c=16)
        lng_s = A.f32(2048)
        lnb_s = A.f32(2048)
        yT = [A.bf16(16 * 128) for _ in range(2)]
        xo = [A.f32(2048) for _ in range(2)]
        z = [A.f32(2048) for _ in range(2)]
        junk2 = A.f32(2048)
        d1 = [A.f32(1) for _ in range(2)]
        d2 = [A.f32(1) for _ in range(2)]
        dmu = [A.f32(1) for _ in range(2)]
        dvar = [A.f32(1) for _ in range(2)]
        drs = [A.f32(1) for _ in range(2)]
        P.dma("sp", lng_s, lng.partition_broadcast(128), writes=["lng"], semkey="lng")
        P.dma("sp", lnb_s, lnb.partition_broadcast(128), writes=["lnb"], semkey="lnb")
        load_weights(stage, Wo, w_out, 0, 2048, "Wo")
        Yscr_c = Yscr.rearrange("(c p) t -> p c t", p=128)
        NTD = NOWN // 128
        def d_loads(t):
            k2 = t % 2
            P.dma("sp", yT[k2].rearrange("p (c t) -> p c t", c=16), Yscr_c[:, :, t * 128:(t + 1) * 128], writes=[("yT", k2)], semkey=("yT", k2))
            P.dma("sp", xo[k2], x_own[t * 128:(t + 1) * 128, :], writes=[("xo_d", k2)], semkey=("xo_d", k2))

        d_loads(0)
        for t in range(NTD):
            k2 = t % 2
            if t + 1 < NTD:
                d_loads(t + 1)
            yv = yT[k2].rearrange("p (c t) -> p c t", c=16)
            for q4 in range(4):
                b_ = (t % 2) * 4 + q4
                ps = pbank[b_][:]
                mm_group(ps, ("ps", b_), [(yv[:, ec, :], Wo_v[:, ec, q4 * 512:(q4 + 1) * 512], [("yT", k2), ("Wo", ec, 0)]) for ec in range(16)])
                P.op("dve", lambda e, ps=ps, k2=k2, q4=q4: e.scalar_tensor_tensor(out=z[k2][:, q4 * 512:(q4 + 1) * 512], in0=xo[k2][:, q4 * 512:(q4 + 1) * 512],
                                                                                  scalar=ALPHA, in1=ps, op0=ALU.mult, op1=ALU.add),
                     [("ps", b_), ("xo_d", k2)], [("z", k2, q4)])
            zk = [("z", k2, q4) for q4 in range(4)]
            P.op("dve", lambda e, k2=k2: e.tensor_reduce(out=d1[k2], in_=z[k2], axis=AX.X, op=ALU.add), zk, [("d1", k2)])
            P.op("dve", lambda e, k2=k2: e.scalar_tensor_tensor(out=junk2, in0=z[k2], scalar=1.0, in1=z[k2], op0=ALU.mult, op1=ALU.mult,
                                                                accum_out=d2[k2]), zk, ["junk2", ("d2", k2)])
            P.op("dve", lambda e, k2=k2: e.tensor_scalar(out=dmu[k2], in0=d1[k2], scalar1=1.0 / 2048.0, scalar2=None, op0=ALU.mult), [("d1", k2)], [("dmu", k2)])
            P.op("dve", lambda e, k2=k2: e.tensor_tensor(out=dvar[k2], in0=dmu[k2], in1=dmu[k2], op=ALU.mult), [("dmu", k2)], [("dvar", k2)])
            P.op("dve", lambda e, k2=k2: e.scalar_tensor_tensor(out=dvar[k2], in0=d2[k2], scalar=1.0 / 2048.0, in1=dvar[k2], op0=ALU.mult,
                                                                op1=ALU.subtract), [("d2", k2), ("dvar", k2)], [("dvar", k2)])
            P.op("act", lambda e, k2=k2: e.activation(out=drs[k2], in_=dvar[k2], func=AF.Sqrt, bias=eps_s[:, 0:1], scale=1.0), [("dvar", k2), "eps"], [("drs", k2)])
            P.op("dve", lambda e, k2=k2: e.reciprocal(out=drs[k2], in_=drs[k2]), [("drs", k2)], [("drs", k2)])
            P.op("dve", lambda e, k2=k2: e.tensor_scalar(out=z[k2], in0=z[k2], scalar1=dmu[k2][:, 0:1], scalar2=drs[k2][:, 0:1], op0=ALU.subtract,
                                                         op1=ALU.mult), zk + [("dmu", k2), ("drs", k2)], zk)
            P.op("pool", lambda e, k2=k2: e.tensor_tensor(out=z[k2], in0=z[k2], in1=lng_s, op=ALU.mult), zk + ["lng"], zk)
            P.op("pool", lambda e, k2=k2: e.tensor_tensor(out=z[k2], in0=z[k2], in1=lnb_s, op=ALU.add), zk + ["lnb"], zk)
            P.dma("pool", out[t * 128:(t + 1) * 128, :], z[k2], reads=zk, writes=[("out", t)], semkey=("zo", k2))
        P.op("sp", None, reads=[("out", t) for t in range(NTD)])
        P.barrier()
        P.emit()
    return nc


def make_in_maps(x, mem, positions, w_in, w_mem_kv, gmlp_ln_g, gmlp_ln_b, gmlp_w_s, gmlp_b_s, w_out, ln_g, ln_b):
    B, T, D = x.shape
    NB = T // 256
    NI = NB // 4
    f32 = np.float32
    x = np.asarray(x, f32)
    mem = np.asarray(mem, f32)
    positions = np.asarray(positions).astype(np.int32)
    w_in0 = np.ascontiguousarray(np.asarray(w_in, f32)[0])
    w_out0 = np.ascontiguousarray(np.asarray(w_out, f32)[0])
    w_mkv0 = np.ascontiguousarray(np.asarray(w_mem_kv, f32)[0])
    ws = np.asarray(gmlp_w_s, f32)[0]
    wsT = np.ascontiguousarray(ws.transpose(2, 0, 1)).reshape(128, 512)
    bs = np.ascontiguousarray(np.asarray(gmlp_b_s, f32)[0].reshape(1, 512))
    half = 64
    inv = (10000.0 ** (-np.arange(half, dtype=np.float32) / half)).astype(f32)
    invf = np.concatenate([inv, inv]).reshape(128, 1).astype(f32)
    in_maps = []
    idx_list = []
    xT_cache = {}
    for c in range(8):
        b, jc = divmod(c, 4)
        own_blocks = [4 * i + jc for i in range(NI)]
        own_idx = np.concatenate([np.arange(n * 256, (n + 1) * 256) for n in own_blocks])
        idx_list.append((b, own_idx))
        if b not in xT_cache:
            xT_cache[b] = np.ascontiguousarray(x[b].T)
        x_own = np.ascontiguousarray(x[b][own_idx])
        pastneg = np.zeros((NI // 2, 4, NB), f32)
        notown = np.full((NI // 2, 4, NB), NEG, f32)
        for i, n in enumerate(own_blocks):
            for sb_ in range(2):
                pastneg[i // 2, (i % 2) * 2 + sb_, n:] = -1e30
                notown[i // 2, (i % 2) * 2 + sb_, n] = 0.0
        m2d = np.zeros((128, 8, 2, 512), f32)
        kk = np.arange(128)[:, None]
        qq = np.arange(256)[None, :]
        for r in range(8):
            for cc in range(2):
                for hf in range(2):
                    own_r = hf * 4 + jc
                    blk = m2d[:, r, cc, hf * 256:(hf + 1) * 256]
                    if r > own_r:
                        blk[:] = NEG
                    elif r == own_r:
                        blk[:] = np.where((cc * 128 + kk) > qq, NEG, 0.0)
        in_maps.append(dict(
            xT_all=xT_cache[b], xT_own=np.ascontiguousarray(x_own.T), x_own=x_own,
            pos_all=np.ascontiguousarray(positions[b][None, :]), pos_own=np.ascontiguousarray(positions[b][own_idx][None, :]),
            w_in=w_in0, w_out=w_out0, w_mkv=w_mkv0, memT=np.ascontiguousarray(mem[b].T),
            glng=np.asarray(gmlp_ln_g, f32).reshape(1, 512), glnb=np.asarray(gmlp_ln_b, f32).reshape(1, 512),
            wsT=wsT, bs=bs, lng=np.asarray(ln_g, f32).reshape(1, 2048), lnb=np.asarray(ln_b, f32).reshape(1, 2048),
            invf=invf, pastneg=pastneg.reshape(1, -1), notown=notown.reshape(1, -1), m2d=m2d.reshape(128, -1)))
    return in_maps, idx_list


_NC_CACHE = {}


def kernel(x, mem, positions, w_in, w_mem_kv, gmlp_ln_g, gmlp_ln_b, gmlp_w_s, gmlp_b_s, w_out, ln_g, ln_b):
    x = np.asarray(x)
    B, T, D = x.shape
    in_maps, idx_list = make_in_maps(x, mem, positions, w_in, w_mem_kv, gmlp_ln_g, gmlp_ln_b, gmlp_w_s, gmlp_b_s, w_out, ln_g, ln_b)
    if T not in _NC_CACHE:
        _NC_CACHE[T] = build(T)
    nc = _NC_CACHE[T]
    res = run_bass_kernel_spmd(nc, in_maps, core_ids=list(range(8)))
    outp = np.empty((B, T, D), np.float32)
    for c in range(8):
        b, own_idx = idx_list[c]
        outp[b, own_idx] = res.results[c]["out"]
    return outp
```
